# Optimizing a Trainium2 kernel written in Bass

```python
import math
import jax
import jax.numpy as jnp
from jax import lax
import numpy as np

D_MODEL = 1024
BATCH = 16
SEQ = 4096
DEPTH = 1

CTX_LEN = 256
GRID_W = 64
EPS = 1e-6
N_MOD = 6

MLA_HEADS = 8
Q_LORA = 256
KV_LORA = 128
QK_NOPE = 128
QK_ROPE = 64
V_DIM = 128
ROPE_THETA = 10000.0
ATTN_SCALE = (QK_NOPE + QK_ROPE) ** -0.5
Q_BLOCK = 128

D_INNER = 2 * D_MODEL
SSD_HEADDIM = 64
SSD_HEADS = D_INNER // SSD_HEADDIM
SSD_GROUPS = 4
HEADS_PER_GROUP = SSD_HEADS // SSD_GROUPS
D_STATE = 128
CONV_W = 5
CONV_DIM = D_INNER + 2 * SSD_GROUPS * D_STATE
CHUNK = 128

D_FF = -(-8 * D_MODEL // (3 * 256)) * 256

IN_SIZES = (Q_LORA, KV_LORA, QK_ROPE, D_INNER, CONV_DIM, 2 * SSD_HEADS, 2 * D_MODEL)
D_IN_PROJ = sum(IN_SIZES)

kernel_name = "hybrid_mla_ssd_dit_layer"


def rmsnorm(x, w):
    xf = x.astype(jnp.float32)
    xf = xf * lax.rsqrt(jnp.mean(xf * xf, axis=-1, keepdims=True) + EPS)
    return (xf * w.astype(jnp.float32)).astype(x.dtype)


def modulate(x, shift, scale):
    return x * (1 + scale) + shift


def split_in(p):
    idx = np.cumsum(IN_SIZES)[:-1].tolist()
    return jnp.split(p, idx, axis=-1)


def axial_rope(rows, dtype):
    row_pos = jnp.broadcast_to(jnp.arange(rows)[:, None], (rows, GRID_W)).reshape(-1)
    col_pos = jnp.broadcast_to(jnp.arange(GRID_W)[None, :], (rows, GRID_W)).reshape(-1)
    n_freq = QK_ROPE // 4
    freqs = ROPE_THETA ** (-jnp.arange(n_freq, dtype=jnp.float32) / n_freq)
    ang = jnp.concatenate([row_pos[:, None] * freqs, col_pos[:, None] * freqs], axis=-1)
    return jnp.cos(ang).astype(dtype), jnp.sin(ang).astype(dtype)


def apply_rope(x, cos, sin):
    x1, x2 = jnp.split(x, 2, axis=-1)
    return jnp.concatenate([x1 * cos - x2 * sin, x2 * cos + x1 * sin], axis=-1)


def mla_project(cq, ckv, w_q_norm, w_uq, w_kv_norm, w_ukv):
    b, n, _ = cq.shape
    q = (rmsnorm(cq, w_q_norm) @ w_uq).reshape(b, n, MLA_HEADS, QK_NOPE + QK_ROPE)
    kv = (rmsnorm(ckv, w_kv_norm) @ w_ukv).reshape(b, n, MLA_HEADS, QK_NOPE + V_DIM)
    return q[..., :QK_NOPE], q[..., QK_NOPE:], kv[..., :QK_NOPE], kv[..., QK_NOPE:]


def attend(q_nope, q_rope, k_nope, k_rope, v):
    s = jnp.einsum("bqhd,bkhd->bhqk", q_nope, k_nope) + jnp.einsum("bqhr,bkr->bhqk", q_rope, k_rope)
    p = jax.nn.softmax(s.astype(jnp.float32) * ATTN_SCALE, axis=-1).astype(v.dtype)
    return jnp.einsum("bhqk,bkhd->bqhd", p, v)


def attend_latent(q_nope, q_rope, k_nope, k_rope, v):
    b, s, h, _ = q_nope.shape
    nb = s // Q_BLOCK

    def to_blocks(t):
        return jnp.swapaxes(t.reshape(b, nb, Q_BLOCK, *t.shape[2:]), 0, 1)

    def one_block(qs):
        return attend(qs[0], qs[1], k_nope, k_rope, v)

    o = lax.map(one_block, (to_blocks(q_nope), to_blocks(q_rope)))
    return jnp.swapaxes(o, 0, 1).reshape(b, s, h * V_DIM)


def dwconv_centered(u, w, bias):
    y = lax.conv_general_dilated(
        u, w[:, None, :].astype(u.dtype), window_strides=(1,),
        padding=((CONV_W // 2, CONV_W // 2),),
        dimension_numbers=("NWC", "WIO", "NWC"),
        feature_group_count=u.shape[-1])
    return y + bias.astype(u.dtype)


def segsum(a):
    t = a.shape[-1]
    cs = jnp.cumsum(a, axis=-1)
    diff = cs[..., :, None] - cs[..., None, :]
    mask = jnp.tril(jnp.ones((t, t), dtype=bool))
    return jnp.where(mask, diff, -jnp.inf)


def ssd_scan(xs, dt, a, bm, cm, init):
    b, l, g, r, p = xs.shape
    n = bm.shape[-1]
    nc = l // CHUNK
    xd = (xs * dt[..., None].astype(xs.dtype)).reshape(b, nc, CHUNK, g, r, p)
    ad = jnp.moveaxis((dt * a).reshape(b, nc, CHUNK, g, r), (3, 4), (1, 2))
    bc = bm.reshape(b, nc, CHUNK, g, n)
    cc = cm.reshape(b, nc, CHUNK, g, n)
    a_cs = jnp.cumsum(ad, axis=-1)
    lmat = jnp.exp(segsum(ad)).astype(xs.dtype)
    y_diag = jnp.einsum("bclgn,bcsgn,bgrcls,bcsgrp->bclgrp", cc, bc, lmat, xd)
    decay_states = jnp.exp(a_cs[..., -1:] - a_cs).astype(xs.dtype)
    states = jnp.einsum("bclgn,bgrcl,bclgrp->bcgrpn", bc, decay_states, xd)
    states = jnp.concatenate([init[:, None].astype(states.dtype), states], axis=1)
    chunk_a = jnp.pad(a_cs[..., -1], ((0, 0), (0, 0), (0, 0), (1, 0)))
    chunk_decay = jnp.exp(segsum(chunk_a)).astype(xs.dtype)
    states = jnp.einsum("bgrzc,bcgrpn->bzgrpn", chunk_decay, states)
    prev_states, final_state = states[:, :-1], states[:, -1]
    y_off = jnp.einsum("bclgn,bcgrpn,bgrcl->bclgrp", cc, prev_states, jnp.exp(a_cs).astype(xs.dtype))
    return (y_diag + y_off).reshape(b, l, g, r, p), final_state


def ssd_branch(z, xbc, dtr, zc, xbcc, dtrc, with_ctx_out, conv_w, conv_b, dt_bias, a_log, d_skip, w_ssd_norm):
    b = xbc.shape[0]
    a = -jnp.exp(a_log.astype(jnp.float32)).reshape(2, SSD_GROUPS, HEADS_PER_GROUP)
    dtb = dt_bias.astype(jnp.float32).reshape(2, SSD_GROUPS, HEADS_PER_GROUP)

    def prep(u, dr):
        n = u.shape[1]
        u = jax.nn.silu(dwconv_centered(u, conv_w, conv_b))
        xs, bm, cm = jnp.split(u, [D_INNER, D_INNER + SSD_GROUPS * D_STATE], axis=-1)
        dt = jax.nn.softplus(dr.astype(jnp.float32).reshape(b, n, 2, SSD_GROUPS, HEADS_PER_GROUP) + dtb)
        return (xs.reshape(b, n, SSD_GROUPS, HEADS_PER_GROUP, SSD_HEADDIM),
                bm.reshape(b, n, SSD_GROUPS, D_STATE), cm.reshape(b, n, SSD_GROUPS, D_STATE), dt)

    xs, bm, cm, dt = prep(xbc, dtr)
    xsc, bmc, cmc, dtc = prep(xbcc, dtrc)
    zero = jnp.zeros((b, SSD_GROUPS, HEADS_PER_GROUP, SSD_HEADDIM, D_STATE), jnp.float32)

    def flip(u):
        return jnp.flip(u, axis=1)

    yc_f, hc_f = ssd_scan(xsc, dtc[:, :, 0], a[0], bmc, cmc, zero)
    y_f, _ = ssd_scan(xs, dt[:, :, 0], a[0], bm, cm, hc_f)
    yc_b, hc_b = ssd_scan(flip(xsc), flip(dtc[:, :, 1]), a[1], flip(bmc), flip(cmc), zero)
    y_b, _ = ssd_scan(flip(xs), flip(dt[:, :, 1]), a[1], flip(bm), flip(cm), hc_b)
    d = d_skip.reshape(SSD_GROUPS, HEADS_PER_GROUP, 1)

    def finish(yf, yb, x_, z_):
        y = (yf + yb + d * x_).reshape(b, -1, D_INNER)
        return rmsnorm(y * jax.nn.silu(z_), w_ssd_norm)

    y = finish(y_f, flip(y_b), xs, z)
    yc = finish(yc_f, flip(yc_b), xsc, zc) if with_ctx_out else None
    return y, yc


def token_mixers(h, hc, with_ctx_out, w_in, w_q_norm, w_uq, w_kv_norm, w_ukv, conv_w, conv_b, dt_bias,
                 a_log, d_skip, w_ssd_norm, w_o_mla, w_o_ssd, w_out, cos, sin):
    b, t, _ = hc.shape
    cq, ckv, kr, z, xbc, dtr, gates = split_in(h @ w_in)
    cqc, ckvc, krc, zc, xbcc, dtrc, gatesc = split_in(hc @ w_in)

    qn, qr, kn, v = mla_project(cq, ckv, w_q_norm, w_uq, w_kv_norm, w_ukv)
    qr = apply_rope(qr, cos[:, None, :], sin[:, None, :])
    kr = apply_rope(kr, cos, sin)
    qnc, qrc, knc, vc = mla_project(cqc, ckvc, w_q_norm, w_uq, w_kv_norm, w_ukv)
    y_mla = attend_latent(qn, qr, jnp.concatenate([kn, knc], axis=1),
                          jnp.concatenate([kr, krc], axis=1), jnp.concatenate([v, vc], axis=1))

    y_ssd, yc_ssd = ssd_branch(z, xbc, dtr, zc, xbcc, dtrc, with_ctx_out,
                               conv_w, conv_b, dt_bias, a_log, d_skip, w_ssd_norm)

    g_mla, g_ssd = jnp.split(jax.nn.sigmoid(gates), 2, axis=-1)
    out = (g_mla * (y_mla @ w_o_mla) + g_ssd * (y_ssd @ w_o_ssd)) @ w_out
    out_c = None
    if with_ctx_out:
        yc_mla = attend(qnc, qrc, knc, krc, vc).reshape(b, t, MLA_HEADS * V_DIM)
        gc_mla, gc_ssd = jnp.split(jax.nn.sigmoid(gatesc), 2, axis=-1)
        out_c = (gc_mla * (yc_mla @ w_o_mla) + gc_ssd * (yc_ssd @ w_o_ssd)) @ w_out
    return out, out_c


def swiglu(h, w_ffn_in, w_ffn_down):
    gate, up = jnp.split(h @ w_ffn_in, 2, axis=-1)
    return (jax.nn.silu(gate) * up) @ w_ffn_down


def setup_inputs(seed: int = 0) -> dict:
    key = jax.random.key(seed)
    ks = jax.random.split(key, 26)
    f32 = jnp.float32
    L = DEPTH

    def nrm(k, shape, scale):
        return jax.random.normal(k, shape, f32) * scale

    def gain(k, shape):
        return 1.0 + 0.01 * jax.random.normal(k, shape, f32)

    dt0 = jnp.exp(jax.random.uniform(ks[14], (L, 2, SSD_HEADS), f32, math.log(1e-3), math.log(1e-1)))
    dt_bias = dt0 + jnp.log(-jnp.expm1(-dt0))
    a_log = jnp.log(jax.random.uniform(ks[15], (L, 2, SSD_HEADS), f32, 1.0, 16.0))
    return {
        "x": nrm(ks[0], (BATCH, SEQ, D_MODEL), 1.0),
        "c": nrm(ks[1], (BATCH, D_MODEL), 1.0),
        "ctx": nrm(ks[2], (BATCH, CTX_LEN, D_MODEL), 1.0),
        "c_ctx": nrm(ks[3], (D_MODEL,), 1.0),
        "w_ada": nrm(ks[4], (L, D_MODEL, N_MOD * D_MODEL), D_MODEL ** -0.5),
        "b_ada": nrm(ks[5], (L, N_MOD * D_MODEL), 0.01),
        "w_norm_mix": gain(ks[6], (L, D_MODEL)),
        "w_in": nrm(ks[7], (L, D_MODEL, D_IN_PROJ), D_MODEL ** -0.5),
        "w_q_norm": gain(ks[8], (L, Q_LORA)),
        "w_uq": nrm(ks[9], (L, Q_LORA, MLA_HEADS * (QK_NOPE + QK_ROPE)), Q_LORA ** -0.5),
        "w_kv_norm": gain(ks[10], (L, KV_LORA)),
        "w_ukv": nrm(ks[11], (L, KV_LORA, MLA_HEADS * (QK_NOPE + V_DIM)), KV_LORA ** -0.5),
        "conv_w": nrm(ks[12], (L, CONV_W, CONV_DIM), CONV_W ** -0.5),
        "conv_b": nrm(ks[13], (L, CONV_DIM), 0.01),
        "dt_bias": dt_bias,
        "a_log": a_log,
        "d_skip": gain(ks[16], (L, SSD_HEADS)),
        "w_ssd_norm": gain(ks[17], (L, D_INNER)),
        "w_o_mla": nrm(ks[18], (L, MLA_HEADS * V_DIM, D_MODEL), (MLA_HEADS * V_DIM) ** -0.5),
        "w_o_ssd": nrm(ks[19], (L, D_INNER, D_MODEL), D_INNER ** -0.5),
        "w_out": nrm(ks[20], (L, D_MODEL, D_MODEL), D_MODEL ** -0.5),
        "w_norm_ffn": gain(ks[21], (L, D_MODEL)),
        "w_ffn_in": nrm(ks[22], (L, D_MODEL, 2 * D_FF), D_MODEL ** -0.5),
        "w_ffn_down": nrm(ks[23], (L, D_FF, D_MODEL), D_FF ** -0.5),
        "w_norm_final": gain(ks[24], (D_MODEL,)),
    }


def reference(x, c, ctx, c_ctx, w_ada, b_ada, w_norm_mix, w_in, w_q_norm, w_uq, w_kv_norm, w_ukv,
              conv_w, conv_b, dt_bias, a_log, d_skip, w_ssd_norm, w_o_mla, w_o_ssd, w_out,
              w_norm_ffn, w_ffn_in, w_ffn_down, w_norm_final):
    rows = x.shape[1] // GRID_W
    cos, sin = axial_rope(rows, x.dtype)
    xc = ctx
    for l in range(DEPTH):
        is_last = l == DEPTH - 1
        mod = jax.nn.silu(c) @ w_ada[l] + b_ada[l]
        mod_c = jax.nn.silu(c_ctx) @ w_ada[l] + b_ada[l]
        sh1, sc1, g1, sh2, sc2, g2 = jnp.split(mod[:, None, :], N_MOD, axis=-1)
        sh1c, sc1c, g1c, sh2c, sc2c, g2c = jnp.split(mod_c, N_MOD, axis=-1)

        h = modulate(rmsnorm(x, w_norm_mix[l]), sh1, sc1)
        hc = modulate(rmsnorm(xc, w_norm_mix[l]), sh1c, sc1c)
        mix, mix_c = token_mixers(h, hc, not is_last, w_in[l], w_q_norm[l], w_uq[l], w_kv_norm[l], w_ukv[l],
                                  conv_w[l], conv_b[l], dt_bias[l], a_log[l], d_skip[l], w_ssd_norm[l],
                                  w_o_mla[l], w_o_ssd[l], w_out[l], cos, sin)
        x = x + g1 * mix
        h = modulate(rmsnorm(x, w_norm_ffn[l]), sh2, sc2)
        x = x + g2 * swiglu(h, w_ffn_in[l], w_ffn_down[l])
        if not is_last:
            xc = xc + g1c * mix_c
            hc = modulate(rmsnorm(xc, w_norm_ffn[l]), sh2c, sc2c)
            xc = xc + g2c * swiglu(hc, w_ffn_in[l], w_ffn_down[l])
    return rmsnorm(x, w_norm_final)
```

```python
import contextlib
import numpy as np
import concourse.bass as bass
import concourse.mybir as mybir
from concourse.bass_utils import run_bass_kernel_spmd

F32 = mybir.dt.float32
BF16 = mybir.dt.bfloat16
AF = mybir.ActivationFunctionType
ALU = mybir.AluOpType

D = 1024
KC = 8
CTX = 256
GRID_W = 64
EPS = 1e-6
H = 8
QL = 256
KVL = 128
DN = 128
DR = 64
DV = 128
ATTN_SCALE = (DN + DR) ** -0.5
DI = 2048
SH = 32
SG = 4
HPG = 8
DS = 128
CW = 5
CDIM = 3072
DFF = 2816
NFF = 22
IN_SIZES = (QL, KVL, DR, DI, CDIM, 2 * SH, 2 * D)
OFF_Q, OFF_KV, OFF_KR, OFF_Z, OFF_X, OFF_DT, OFF_G = [int(v) for v in np.cumsum((0,) + IN_SIZES)[:-1]]
DIN = sum(IN_SIZES)
OFF_KRS = DIN


class Buf:
    __slots__ = ("name", "w", "rs")

    def __init__(self, name):
        self.name = name
        self.w = None
        self.rs = []


class Op:
    __slots__ = ("eng", "fn", "deps", "sig", "sigval", "dma", "sem", "semval")

    def __init__(self, eng, fn, dma):
        self.eng = eng
        self.fn = fn
        self.deps = []
        self.sig = False
        self.sigval = 0
        self.dma = dma
        self.sem = None
        self.semval = 0


class Prog:
    ENGS = ("pe", "act", "dve", "pool", "sp")
    NDMA = 24

    def __init__(self, nc):
        self.nc = nc
        self.ops = {e: [] for e in self.ENGS}
        self.dma_count = {e: 0 for e in self.ENGS}
        self.dma_last = {}
        self.out_dmas = []

    def add(self, eng, fn, reads=(), writes=(), dma=False):
        op = Op(eng, fn, dma)
        deps = []
        for b in reads:
            if b.w is not None:
                deps.append((b.w, True))
        for b in writes:
            if b.w is not None:
                deps.append((b.w, False))
            for r in b.rs:
                deps.append((r, False))
        for b in writes:
            b.w = op
            b.rs = []
        for b in reads:
            b.rs.append(op)
        if dma:
            k = self.dma_count[eng]
            self.dma_count[eng] = k + 1
            slot = (eng, k % self.NDMA)
            prev = self.dma_last.get(slot)
            if prev is not None:
                deps.append((prev, True))
            self.dma_last[slot] = op
            op.sem = slot
            op.semval = 16 * (k // self.NDMA + 1)
        seen = set()
        for d, raw in deps:
            if d is op or id(d) in seen:
                continue
            if (not d.dma) and (not dma) and d.eng == eng:
                if eng == "pe" or not raw:
                    continue
            seen.add(id(d))
            op.deps.append(d)
        self.ops[eng].append(op)
        return op

    def barrier(self):
        lasts = []
        for e in self.ENGS:
            comp = [o for o in self.ops[e] if not o.dma]
            if comp:
                lasts.append(comp[-1])
        lasts.extend(self.dma_last.values())
        for e in self.ENGS:
            op = Op(e, lambda eng: eng.nop(), False)
            for d in lasts:
                if d.dma or d.eng != e:
                    op.deps.append(d)
            self.ops[e].append(op)

    def emit(self):
        nc = self.nc
        for e in self.ENGS:
            for op in self.ops[e]:
                for d in op.deps:
                    if not d.dma:
                        d.sig = True
        for e in self.ENGS:
            c = 0
            for op in self.ops[e]:
                if op.sig and not op.dma:
                    c += 1
                    op.sigval = c
        with contextlib.ExitStack() as es:
            csem = {e: es.enter_context(nc.semaphore("c_" + e)) for e in self.ENGS}
            dsem = {}
            for e in self.ENGS:
                for i in range(min(self.NDMA, self.dma_count[e])):
                    dsem[(e, i)] = es.enter_context(nc.semaphore("d_%s_%d" % (e, i)))
            block = es.enter_context(nc.Block())

            def run(e, eng):
                waited = {}
                for op in self.ops[e]:
                    need = {}
                    for d in op.deps:
                        if d.dma:
                            key = ("d",) + d.sem
                            s = dsem[d.sem]
                            v = d.semval
                        else:
                            key = ("c", d.eng)
                            s = csem[d.eng]
                            v = d.sigval
                        if need.get(key, (None, 0))[1] < v:
                            need[key] = (s, v)
                    for key, (s, v) in need.items():
                        if waited.get(key, 0) >= v:
                            continue
                        waited[key] = v
                        eng.wait_ge(s, v)
                    ins = op.fn(eng)
                    if op.dma:
                        ins.then_inc(dsem[op.sem], 16)
                    elif op.sig:
                        ins.then_inc(csem[e], 1)

            block.tensor(lambda eng: run("pe", eng))
            block.scalar(lambda eng: run("act", eng))
            block.vector(lambda eng: run("dve", eng))
            block.gpsimd(lambda eng: run("pool", eng))
            block.sync(lambda eng: run("sp", eng))


class Ring:
    def __init__(self, items):
        self.items = items
        self.i = 0

    def next(self):
        it = self.items[self.i % len(self.items)]
        self.i += 1
        return it


def _kc_layout(w):
    K, N = w.shape
    return np.ascontiguousarray(w.reshape(K // 128, 128, N).transpose(1, 0, 2))


def _fm_layout(v):
    return np.ascontiguousarray(v.reshape(-1, 128).T)


def _rope_tables(S):
    rows = S // GRID_W
    row_pos = np.broadcast_to(np.arange(rows)[:, None], (rows, GRID_W)).reshape(-1).astype(np.float32)
    col_pos = np.broadcast_to(np.arange(GRID_W)[None, :], (rows, GRID_W)).reshape(-1).astype(np.float32)
    n_freq = DR // 4
    freqs = (np.float32(10000.0) ** (-np.arange(n_freq, dtype=np.float32) / np.float32(n_freq))).astype(np.float32)
    ang = np.concatenate([row_pos[:, None] * freqs, col_pos[:, None] * freqs], axis=-1).astype(np.float32)
    cos = np.cos(ang).astype(np.float32).T
    sin = np.sin(ang).astype(np.float32).T
    cs = np.stack([np.concatenate([cos, cos], 0), np.concatenate([-sin, sin], 0)], axis=1)
    return np.ascontiguousarray(cs.astype(np.float32))


def _consts(S):
    k = np.arange(128)
    c = {}
    c["ident"] = np.eye(128, dtype=np.float32)
    tri = np.zeros((128, 4, 128), np.float32)
    tri[:, 0, :] = (k[:, None] > k[None, :])
    tri[:, 1, :] = (k[:, None] <= k[None, :])
    tri[:, 2, :] = (k[:, None] < k[None, :])
    tri[:, 3, :] = (k[:, None] >= k[None, :])
    c["tri"] = tri
    c["rope"] = _rope_tables(S)
    return c


def prep_shared(inp, S):
    g = {}
    L = 0
    w_in = np.asarray(inp["w_in"][L], np.float32)
    kr = w_in[:, OFF_KR:OFF_KR + DR]
    w_in_x = np.concatenate([w_in, kr[:, 32:64], kr[:, 0:32]], axis=1)
    g["w_in"] = _kc_layout(w_in_x)
    g["w_ada"] = _kc_layout(np.asarray(inp["w_ada"][L], np.float32))
    b_ada = np.asarray(inp["b_ada"][L], np.float32)
    g["b_ada_fm"] = _fm_layout(b_ada)
    g["b_ada_row"] = b_ada.reshape(1, -1)
    g["wnorm_fm"] = np.ascontiguousarray(np.stack([_fm_layout(np.asarray(inp["w_norm_mix"][L], np.float32)),
                                                   _fm_layout(np.asarray(inp["w_norm_ffn"][L], np.float32))], axis=2))
    g["wq_fm"] = _fm_layout(np.asarray(inp["w_q_norm"][L], np.float32))
    g["wkv_fm"] = _fm_layout(np.asarray(inp["w_kv_norm"][L], np.float32))
    w_uq = np.asarray(inp["w_uq"][L], np.float32).reshape(QL, H, DN + DR)
    rope = w_uq[:, :, DN:]
    w_uq_x = np.concatenate([w_uq.reshape(QL, H * (DN + DR)),
                             np.concatenate([rope[:, :, 32:], rope[:, :, :32]], axis=2).reshape(QL, H * DR)], axis=1)
    g["w_uq"] = _kc_layout(w_uq_x)
    w_ukv = np.asarray(inp["w_ukv"][L], np.float32).reshape(KVL, H, DN + DV)
    g["w_ukT"] = np.ascontiguousarray(w_ukv[:, :, :DN].transpose(2, 1, 0))
    g["w_uv"] = np.ascontiguousarray(w_ukv[:, :, DN:])
    conv_w = np.asarray(inp["conv_w"][L], np.float32)
    g["convw_fm"] = np.ascontiguousarray(conv_w.T.reshape(24, 128, CW).transpose(1, 0, 2))
    conv_b = np.asarray(inp["conv_b"][L], np.float32)
    g["convb_fm"] = _fm_layout(conv_b)
    g["convb_row"] = conv_b.reshape(1, -1)
    g["dtb_row"] = np.asarray(inp["dt_bias"][L], np.float32).reshape(1, 64)
    g["alog_row"] = np.asarray(inp["a_log"][L], np.float32).reshape(1, 64)
    g["dskip_row"] = np.asarray(inp["d_skip"][L], np.float32).reshape(1, 32)
    g["wssd_row"] = np.asarray(inp["w_ssd_norm"][L], np.float32).reshape(1, DI)
    g["wfinal_row"] = np.asarray(inp["w_norm_final"], np.float32).reshape(1, D)
    g["w_o_mla"] = _kc_layout(np.asarray(inp["w_o_mla"][L], np.float32))
    g["w_o_ssd"] = _kc_layout(np.asarray(inp["w_o_ssd"][L], np.float32))
    g["w_out"] = _kc_layout(np.asarray(inp["w_out"][L], np.float32))
    g["w_ffn_in"] = _kc_layout(np.asarray(inp["w_ffn_in"][L], np.float32))
    g["w_down"] = _kc_layout(np.asarray(inp["w_ffn_down"][L], np.float32))
    g.update(_consts(S))
    return g


def prep_core(inp, core, nseq):
    b0 = core * nseq
    m = {}
    m["x"] = np.ascontiguousarray(np.asarray(inp["x"][b0:b0 + nseq], np.float32))
    m["ctx"] = np.ascontiguousarray(np.asarray(inp["ctx"][b0:b0 + nseq], np.float32))
    cs = [np.asarray(inp["c"][b0 + i], np.float32) for i in range(nseq)] + [np.asarray(inp["c_ctx"], np.float32)]
    cT = np.stack([_fm_layout(v) for v in cs], axis=2)
    m["cT"] = np.ascontiguousarray(cT)
    return m


class K:
    def __init__(self, nc):
        self.nc = nc
        self.P = Prog(nc)

    def dma(self, out, in_, reads=(), writes=(), q="sp"):
        return self.P.add(q, lambda e: e.dma_start(out=out, in_=in_), reads, writes, dma=True)

    def mm(self, specs, reads, writes):
        def fn(e):
            ins = None
            for (o, l, r, st, sp) in specs:
                ins = e.matmul(o, lhsT=l, rhs=r, start=st, stop=sp)
            return ins
        return self.P.add("pe", fn, reads, writes)

    def tr(self, specs, ident, reads, writes):
        def fn(e):
            ins = None
            for (o, i) in specs:
                ins = e.transpose(o, i, ident)
            return ins
        return self.P.add("pe", fn, reads, writes)

    def act(self, out, in_, func, reads, writes, scale=None, bias=None, accum=None):
        kw = {}
        if scale is not None:
            kw["scale"] = scale
        if bias is not None:
            kw["bias"] = bias
        if accum is not None:
            kw["accum_out"] = accum
        return self.P.add("act", lambda e: e.activation(out=out, in_=in_, func=func, **kw), reads, writes)

    def tt(self, eng, out, in0, in1, op, reads, writes):
        return self.P.add(eng, lambda e: e.tensor_tensor(out=out, in0=in0, in1=in1, op=op), reads, writes)

    def ts(self, out, in0, s1, s2, op0, op1, reads, writes):
        if s2 is None:
            return self.P.add("dve", lambda e: e.tensor_scalar(out=out, in0=in0, scalar1=s1, scalar2=None, op0=op0), reads, writes)
        return self.P.add("dve", lambda e: e.tensor_scalar(out=out, in0=in0, scalar1=s1, scalar2=s2, op0=op0, op1=op1), reads, writes)

    def stt(self, eng, out, in0, scalar, in1, op0, op1, reads, writes):
        return self.P.add(eng, lambda e: e.scalar_tensor_tensor(out=out, in0=in0, scalar=scalar, in1=in1, op0=op0, op1=op1), reads, writes)

    def copy(self, eng, out, in_, reads, writes):
        if eng == "act":
            return self.act(out, in_, AF.Copy, reads, writes)
        return self.P.add(eng, lambda e: e.tensor_copy(out=out, in_=in_), reads, writes)

    def recip(self, out, in_, reads, writes):
        return self.P.add("dve", lambda e: e.reciprocal(out=out, in_=in_), reads, writes)

    def memset(self, eng, ap, val, writes):
        return self.P.add(eng, lambda e: e.memset(ap, val), (), writes)


def build_program(nseq, S, dbg=()):
    assert S % 512 == 0
    NL = S // 128
    NCH = 2 + NL
    NT = CTX + S
    nc = bass.Bass("TRN2", target_bir_lowering=False)
    k = K(nc)
    P = k.P
    NB = nseq + 1

    def din(name, shape, dt=F32):
        return nc.dram_tensor(name, list(shape), dt, kind="ExternalInput").ap()

    def dscr(name, shape, dt):
        kind = "ExternalOutput" if name in dbg else "Internal"
        return nc.dram_tensor(name, list(shape), dt, kind=kind).ap()

    x_d = din("x", [nseq, S, D])
    ctx_d = din("ctx", [nseq, CTX, D])
    cT_d = din("cT", [128, KC, NB])
    w_in_d = din("w_in", [128, KC, DIN + DR])
    w_ada_d = din("w_ada", [128, KC, 6 * D])
    b_ada_fm_d = din("b_ada_fm", [128, 48])
    b_ada_row_d = din("b_ada_row", [1, 6 * D])
    wnorm_fm_d = din("wnorm_fm", [128, KC, 2])
    wq_fm_d = din("wq_fm", [128, 2])
    wkv_fm_d = din("wkv_fm", [128, 1])
    w_uq_d = din("w_uq", [128, 2, H * (DN + DR) + H * DR])
    w_ukT_d = din("w_ukT", [128, H, KVL])
    w_uv_d = din("w_uv", [128, H, DV])
    convw_fm_d = din("convw_fm", [128, 24, CW])
    convb_fm_d = din("convb_fm", [128, 24])
    convb_row_d = din("convb_row", [1, CDIM])
    dtb_row_d = din("dtb_row", [1, 64])
    alog_row_d = din("alog_row", [1, 64])
    dskip_row_d = din("dskip_row", [1, 32])
    wssd_row_d = din("wssd_row", [1, DI])
    wfinal_row_d = din("wfinal_row", [1, D])
    w_o_mla_d = din("w_o_mla", [128, KC, D])
    w_o_ssd_d = din("w_o_ssd", [128, 16, D])
    w_out_d = din("w_out", [128, KC, D])
    w_ffn_in_d = din("w_ffn_in", [128, KC, 2 * DFF])
    w_down_d = din("w_down", [128, NFF, D])
    ident_d = din("ident", [128, 128])
    tri_d = din("tri", [128, 4, 128])
    rope_d = din("rope", [64, 2, S])
    out_d = nc.dram_tensor("out", [nseq, S, D], F32, kind="ExternalOutput").ap()

    wb = {}
    for name, src in (("w_in", w_in_d), ("w_uq", w_uq_d), ("w_ukT", w_ukT_d), ("w_uv", w_uv_d),
                      ("w_o_mla", w_o_mla_d), ("w_o_ssd", w_o_ssd_d), ("w_out", w_out_d),
                      ("w_ffn_in", w_ffn_in_d), ("w_down", w_down_d)):
        wb[name] = (dscr(name + "_bf", src.shape, BF16), Buf(name + "_bf"), src)
    uTc = [dscr("uTc%d" % b, [CDIM, CTX + 4], BF16) for b in range(nseq)]
    uTl = [dscr("uTl%d" % b, [CDIM, S + 4], BF16) for b in range(nseq)]
    sz_s = [dscr("sz%d" % b, [S, DI], BF16) for b in range(nseq)]
    gT_s = [dscr("gT%d" % b, [2 * D, S], BF16) for b in range(nseq)]
    Qn_s = [dscr("Qn%d" % b, [H, 128, S], BF16) for b in range(nseq)]
    Qr_s = [dscr("Qr%d" % b, [H, DR, S], BF16) for b in range(nseq)]
    xs_s = [dscr("xs%d" % b, [NT, DI], BF16) for b in range(nseq)]
    Bt_s = [dscr("Bt%d" % b, [NT, 512], BF16) for b in range(nseq)]
    BCf_s = [dscr("BCf%d" % b, [1024, NT], BF16) for b in range(nseq)]
    yb_s = [dscr("yb%d" % b, [S, DI], BF16) for b in range(nseq)]
    ygT_s = [dscr("ygT%d" % b, [DI, S], BF16) for b in range(nseq)]
    ymT_s = [dscr("ymT%d" % b, [D, S], BF16) for b in range(nseq)]
    KTn_s = [dscr("KTn%d" % b, [128, NT], BF16) for b in range(nseq)]
    KTr_s = [dscr("KTr%d" % b, [64, NT], BF16) for b in range(nseq)]
    Vt_s = [dscr("Vt%d" % b, [128, NCH, 128], BF16) for b in range(nseq)]
    gb_s = dscr("gb", [2, nseq, D], F32)
    x1_s = [dscr("x1_%d" % b, [S, D], F32) for b in range(nseq)]

    es = contextlib.ExitStack()

    uid = [0]

    def sb(name, shape, dt, stack=None):
        uid[0] += 1
        t = (stack or es).enter_context(nc.sbuf_tensor("s%d_%s" % (uid[0], name), list(shape), dt))
        return t, Buf(name)

    def ps(name, shape, dt=F32, stack=None):
        uid[0] += 1
        t = (stack or es).enter_context(nc.psum_tensor("p%d_%s" % (uid[0], name), list(shape), dt))
        return t, Buf(name)

    ident_f, b_ident_f = sb("ident_f", [128, 128], F32)
    ident_b, b_ident_b = sb("ident_b", [128, 128], BF16)
    tri_f, b_tri = sb("tri_f", [128, 4, 128], F32)
    tri_b, b_tri_b = sb("tri_b", [128, 4, 128], BF16)
    ones_f, b_ones_f = sb("ones_f", [128, 128], F32)
    ones_b, b_ones_b = sb("ones_b", [128, 128], BF16)
    modT, b_modT = sb("modT", [128, 48, NB], F32)
    sc1e, b_sc1e = sb("sc1e", [128, KC, NB], F32)
    sc2e, b_sc2e = sb("sc2e", [128, KC, NB], F32)
    wnorm_fm, b_wnorm = sb("wnorm_fm", [128, KC, 2], F32)
    wq_fm, b_wq = sb("wq_fm", [128, 2], F32)
    wkv_fm, b_wkv = sb("wkv_fm", [128, 1], F32)
    dsk_bc, b_dsk = sb("dsk_bc", [128, 32], F32)
    dtb_bc, b_dtb = sb("dtb_bc", [128, 64], F32)
    a_bc, b_abc = sb("a_bc", [128, 64], F32)
    convb_fm, b_convb_fm = sb("convb_fm", [128, 24], F32)
    convb_rb, b_convb_rb = sb("convb_rb", [1, CDIM], BF16)
    dtv, b_dtv = sb("dtv", [128, NCH, 64], F32)
    zpad, b_zpad = sb("zpad", [128, 24, 2], BF16)

    k.dma(ident_f[:], ident_d, writes=[b_ident_f])
    k.dma(tri_f[:], tri_d, writes=[b_tri])
    k.copy("dve", ident_b[:], ident_f[:], [b_ident_f], [b_ident_b])
    k.copy("dve", tri_b[:], tri_f[:], [b_tri], [b_tri_b])
    k.memset("dve", ones_f[:], 1.0, [b_ones_f])
    k.memset("dve", ones_b[:], 1.0, [b_ones_b])
    k.memset("dve", zpad[:], 0.0, [b_zpad])
    k.dma(wnorm_fm[:], wnorm_fm_d, writes=[b_wnorm])
    k.dma(wq_fm[:], wq_fm_d, writes=[b_wq])
    k.dma(wkv_fm[:], wkv_fm_d, writes=[b_wkv])
    k.dma(convb_fm[:], convb_fm_d, writes=[b_convb_fm])
    k.dma(convb_rb[:], convb_row_d, writes=[b_convb_rb], q="pool")
    k.dma(dsk_bc[:], dskip_row_d.partition_broadcast(128), writes=[b_dsk])
    k.dma(dtb_bc[:], dtb_row_d.partition_broadcast(128), writes=[b_dtb])
    k.dma(a_bc[:], alog_row_d.partition_broadcast(128), writes=[b_abc])
    k.act(a_bc[:], a_bc[:], AF.Exp, [b_abc], [b_abc])
    k.ts(a_bc[:], a_bc[:], -1.0, None, ALU.mult, None, [b_abc], [b_abc])
    def cast_weights(names):
        for name in names:
            dst, bw, src = wb[name]
            nk = src.shape[1]
            step = max(1, nk // 4)
            for k0 in range(0, nk, step):
                k1 = min(nk, k0 + step)
                k.dma(dst[:, k0:k1, :], src[:, k0:k1, :], writes=[Buf("tmp")], q="pool")

    for b in range(nseq):
        for t, n in ((uTc[b], CTX), (uTl[b], S)):
            v = t.rearrange("(c p) t -> p c t", p=128)
            k.dma(v[:, :, 0:2], zpad[:], reads=[b_zpad])
            k.dma(v[:, :, n + 2:n + 4], zpad[:], reads=[b_zpad])
    P.barrier()

    ph = contextlib.ExitStack()
    wada_t, b_wada_t = sb("wada_t", [128, KC, 1536], F32, ph)
    cT_f, b_cT_f = sb("cT_f", [128, KC, NB], F32, ph)
    cT_s, b_cT_s = sb("cT_s", [128, KC, NB], F32, ph)
    cbc, b_cbc = sb("cbc", [128, KC, nseq, 128], F32, ph)
    g1bc, b_g1bc = sb("g1bc", [128, nseq, D], F32, ph)
    g2bc, b_g2bc = sb("g2bc", [128, nseq, D], F32, ph)
    bada_fm, b_bada_fm = sb("bada_fm", [128, 48], F32, ph)
    bada_bc, b_bada_bc = sb("bada_bc", [128, 2, D], F32, ph)
    pm, b_pm = ps("pm", [128, 48, NB], F32, ph)
    pg = [ps("pg%d" % i, [128, 512], F32, ph) for i in range(2)]
    k.dma(cT_f[:], cT_d, writes=[b_cT_f])
    k.dma(bada_fm[:], b_ada_fm_d, writes=[b_bada_fm])
    k.dma(bada_bc[:, 0, :], b_ada_row_d[:, 2 * D:3 * D].partition_broadcast(128), writes=[b_bada_bc])
    k.dma(bada_bc[:, 1, :], b_ada_row_d[:, 5 * D:6 * D].partition_broadcast(128), writes=[b_bada_bc])
    k.act(cT_s[:], cT_f[:], AF.Silu, [b_cT_f], [b_cT_s])
    for b in range(nseq):
        k.copy("dve", cbc[:, :, b, :], cT_s[:, :, b:b + 1].to_broadcast([128, KC, 128]), [b_cT_s], [b_cbc])
    for q4 in range(4):
        k.dma(wada_t[:], w_ada_d[:, :, q4 * 1536:(q4 + 1) * 1536], writes=[b_wada_t])
        specs = []
        for cc in range(12):
            ch = q4 * 12 + cc
            for kc in range(KC):
                specs.append((pm[:, ch, :], wada_t[:, kc, cc * 128:(cc + 1) * 128], cT_s[:, kc, :], kc == 0, kc == KC - 1))
        k.mm(specs, [b_wada_t, b_cT_s], [b_pm])
        if q4 in (1, 3):
            gdst, gb = (g1bc, b_g1bc) if q4 == 1 else (g2bc, b_g2bc)
            for b in range(nseq):
                for hf in range(2):
                    pt, bp = pg[hf]
                    specs = [(pt[:], cbc[:, kc, b, :], wada_t[:, kc, 512 + hf * 512:1024 + hf * 512], kc == 0, kc == KC - 1)
                             for kc in range(KC)]
                    k.mm(specs, [b_wada_t, b_cbc], [bp])
                    k.tt("dve", gdst[:, b, hf * 512:(hf + 1) * 512], pt[:], bada_bc[:, 0 if q4 == 1 else 1, hf * 512:(hf + 1) * 512],
                         ALU.add, [bp, b_bada_bc], [gb])
    k.tt("dve", modT[:], pm[:], bada_fm[:].unsqueeze(2).to_broadcast([128, 48, NB]), ALU.add, [b_pm, b_bada_fm], [b_modT])
    k.stt("dve", sc1e[:], modT[:, 8:16, :], 1.0, wnorm_fm[:, :, 0:1].to_broadcast([128, KC, NB]), ALU.add, ALU.mult,
          [b_modT, b_wnorm], [b_sc1e])
    k.stt("dve", sc2e[:], modT[:, 32:40, :], 1.0, wnorm_fm[:, :, 1:2].to_broadcast([128, KC, NB]), ALU.add, ALU.mult,
          [b_modT, b_wnorm], [b_sc2e])
    k.dma(gb_s[0:1, :, :], g1bc[0:1, :, :], reads=[b_g1bc])
    k.dma(gb_s[1:2, :, :], g2bc[0:1, :, :], reads=[b_g2bc])
    P.barrier()
    ph.close()

    wdep = {}
    for name in ("w_uq", "w_ukT"):
        dst, _bw, src = wb[name]
        wdep[name] = Buf(name + "_cast")
        k.dma(dst, src, writes=[wdep[name]], q="pool")
    in_groups = [(0, 448), (OFF_KRS, 64), (OFF_DT, 64)] + [(OFF_X + i * 512, 512) for i in range(6)] \
        + [(OFF_Z + i * 512, 512) for i in range(4)] + [(OFF_G + i * 512, 512) for i in range(4)]
    for (c0_, n_) in in_groups:
        dst, _bw, src = wb["w_in"]
        wdep[c0_] = Buf("w_in_cast_%d" % c0_)
        k.dma(dst[:, :, c0_:c0_ + n_], src[:, :, c0_:c0_ + n_], writes=[wdep[c0_]], q="pool")
    cast_weights(("w_uv", "w_o_mla", "w_o_ssd", "w_out", "w_ffn_in", "w_down"))
    eps_t, b_eps = sb("eps_t", [128, 1], F32)
    k.memset("dve", eps_t[:], EPS, [b_eps])
    nhalf, b_nhalf = sb("nhalf", [128, 512], F32)
    k.memset("dve", nhalf[:], -0.5, [b_nhalf])

    def rstd_op(out, in_, n, nfree, reads, writes):
        k.ts(out, in_, 1.0 / n, EPS, ALU.mult, ALU.add, reads, writes)
        k.tt("pool", out, out, nhalf[:, 0:nfree], ALU.pow, list(writes) + [b_nhalf], writes)

    def ring(name, n, shape, dt, stack, psum=False):
        return Ring([(ps if psum else sb)("%s%d" % (name, i), shape, dt, stack) for i in range(n)])

    def tiles_of():
        return [("ctx", 0, 2)] + [("lat", i * 4, 4) for i in range(S // 512)]

    w_in_bf = wb["w_in"][0]

    def phase_A(b):
        ph = contextlib.ExitStack()
        wuq, b_wuq = sb("A_wuq", [128, 2, H * 192 + H * 64], BF16, ph)
        wukT, b_wukT = sb("A_wukT", [128, H, 128], BF16, ph)
        wdt, b_wdt = sb("A_wdt", [128, KC, 64], BF16, ph)
        xt_r = ring("A_xt", 3, [128, D], F32, ph)
        xn_r = ring("A_xn", 2, [128, 4, D], BF16, ph)
        junk, b_junk = sb("A_junk", [128, D], BF16, ph)
        st_r = ring("A_st", 2, [128, 12], F32, ph)
        hT_r = ring("A_hT", 2, [128, KC, 512], BF16, ph)
        wg_r = ring("A_wg", 4, [128, KC, 512], BF16, ph)
        cs_r = ring("A_cs", 2, [64, 2, 512], F32, ph)
        cq_sb, b_cq = sb("A_cq", [128, 2, 512], F32, ph)
        sq, b_sq = sb("A_sq", [128, 2, 512], BF16, ph)
        rq, b_rq = sb("A_rq", [128, 512], F32, ph)
        cqn, b_cqn = sb("A_cqn", [128, 2, 512], BF16, ph)
        ckv_sb, b_ckv = sb("A_ckv", [128, 512], F32, ph)
        sqk, b_sqk = sb("A_sqk", [128, 512], BF16, ph)
        rk, b_rk = sb("A_rk", [128, 512], F32, ph)
        qn_r = ring("A_qn", 2, [128, 512], BF16, ph)
        Qn_st, b_Qn_st = sb("A_Qn_st", [128, H, 512], BF16, ph)
        Qr_st, b_Qr_st = sb("A_Qr_st", [64, H, 512], BF16, ph)
        t1_r = ring("A_t1", 2, [64, 512], F32, ph)
        t2_r = ring("A_t2", 2, [64, 512], F32, ph)
        zst_r = ring("A_zst", 3, [128, 512], BF16, ph)
        ust_r = ring("A_ust", 2, [128, 4, 512], BF16, ph)
        gst_r = ring("A_gst", 2, [128, 4, 512], BF16, ph)
        dtt_r = ring("A_dtt", 2, [128, 64], F32, ph)
        acc_r = ring("A_acc", 5, [128, 512], F32, ph, psum=True)
        KTn, b_KTn = sb("KTn", [128, NT], BF16, ph)
        KTr, b_KTr = sb("KTr", [64, NT], BF16, ph)
        Vt, b_Vt = sb("Vt", [128, NCH, 128], BF16, ph)
        pT_r = ring("A_pT", 2, [128, 512], BF16, ph, psum=True)

        k.dma(wuq[:], wb["w_uq"][0], reads=[wdep["w_uq"]], writes=[b_wuq])
        k.dma(wukT[:], wb["w_ukT"][0], reads=[wdep["w_ukT"]], writes=[b_wukT])
        k.dma(wdt[:], w_in_bf[:, :, OFF_DT:OFF_DT + 64], reads=[wdep[OFF_DT]], writes=[b_wdt])
        flip = [0]

        def evac_copy(out, in_, reads, writes):
            flip[0] ^= 1
            return k.copy("act" if flip[0] else "dve", out, in_, reads, writes)

        def fm_specs(pt, M, wt, col0, hT, T):
            return [(pt[0:M, 0:T], wt[:, kc, col0:col0 + M], hT[:, kc, 0:T], kc == 0, kc == KC - 1) for kc in range(KC)]

        TT = [0]

        def rms_bcast(sq_tiles, nfeat, out_r, b_out, reads):
            pr, bpr = acc_r.next()
            T_ = TT[0]
            n = len(sq_tiles)
            k.mm([(pr[:, 0:T_], ones_b[:], s, i == 0, i == n - 1) for i, s in enumerate(sq_tiles)], reads + [b_ones_b], [bpr])
            k.act(out_r[:, 0:T_], pr[:, 0:T_], AF.Sqrt, [bpr, b_eps], [b_out], scale=1.0 / nfeat, bias=eps_t[:, 0:1])
            k.recip(out_r[:, 0:T_], out_r[:, 0:T_], [b_out], [b_out])

        def stage1(kind, i0, nch):
            lat = kind == "lat"
            T_ = nch * 128
            mi = b if lat else nseq
            src = x_d[b] if lat else ctx_d[b]
            t0 = i0 * 128 if lat else 0
            c0 = 2 + i0 if lat else 0
            g0 = c0 * 128
            uT = uTl[b] if lat else uTc[b]
            hT, b_hT = hT_r.next()
            xn, b_xn = xn_r.next()
            st, b_st = st_r.next()
            k.memset("dve", st[:], 0.0, [b_st])
            if lat:
                cs, b_cs = cs_r.next()
                k.dma(cs[:, :, 0:T_], rope_d[:, :, t0:t0 + T_], writes=[b_cs])
            for j in range(nch):
                xt, b_xt = xt_r.next()
                k.dma(xt[:], src[t0 + j * 128:t0 + (j + 1) * 128, :], writes=[b_xt])
                k.act(junk[:], xt[:], AF.Square, [b_xt], [b_junk, b_st], accum=st[:, j:j + 1])
                rstd_op(st[:, 8 + j:9 + j], st[:, j:j + 1], D, 1, [b_st], [b_st])
                k.tt("pool", xn[:, j, :], xt[:], st[:, 8 + j:9 + j].to_broadcast([128, D]), ALU.mult, [b_xt, b_st], [b_xn])
            return dict(lat=lat, T_=T_, mi=mi, t0=t0, c0=c0, g0=g0, uT=uT, hT=hT, b_hT=b_hT, cs=(cs if lat else None), b_cs=(b_cs if lat else None), nch=nch,
                        xn=xn, b_xn=b_xn)

        def stage1b(cx):
            T_ = cx['T_']; mi = cx['mi']; nch = cx['nch']; hT = cx['hT']; b_hT = cx['b_hT']; xn = cx['xn']; b_xn = cx['b_xn']
            for kc in range(KC):
                pT, b_pT = pT_r.next()
                k.tr([(pT[:, j * 128:(j + 1) * 128], xn[:, j, kc * 128:(kc + 1) * 128]) for j in range(nch)], ident_b[:],
                     [b_xn, b_ident_b], [b_pT])
                if kc % 2 == 0:
                    k.act(hT[:, kc, 0:T_], pT[:, 0:T_], AF.Identity, [b_pT, b_sc1e, b_modT], [b_hT],
                          scale=sc1e[:, kc, mi:mi + 1], bias=modT[:, kc, mi:mi + 1])
                else:
                    k.ts(hT[:, kc, 0:T_], pT[:, 0:T_], sc1e[:, kc, mi:mi + 1], modT[:, kc, mi:mi + 1], ALU.mult, ALU.add,
                         [b_pT, b_sc1e, b_modT], [b_hT])

        def stage2(cx, mid_hook):
            lat = cx['lat']; T_ = cx['T_']; mi = cx['mi']; t0 = cx['t0']; c0 = cx['c0']; g0 = cx['g0']; uT = cx['uT']
            hT = cx['hT']; b_hT = cx['b_hT']; cs = cx['cs']; b_cs = cx['b_cs']; nch = cx['nch']
            TT[0] = T_
            wg, b_wg = get_wg()
            pt, bpt = acc_r.next()
            k.mm(fm_specs(pt, 128, wg, OFF_KV, hT, T_), [b_wg, b_hT], [bpt])
            k.act(ckv_sb[:, 0:T_], pt[:, 0:T_], AF.Copy, [bpt], [b_ckv])
            k.act(sqk[:, 0:T_], pt[:, 0:T_], AF.Square, [bpt], [b_sqk])
            rms_bcast([sqk[:, 0:T_]], KVL, rk, b_rk, [b_sqk])
            k.stt("dve", KTn[:, g0:g0 + T_], ckv_sb[:, 0:T_], wkv_fm[:, 0:1], rk[:, 0:T_], ALU.mult, ALU.mult,
                  [b_ckv, b_wkv, b_rk], [b_KTn])
            pT, b_pT = pT_r.next()
            k.tr([(pT[:, j * 128:(j + 1) * 128], KTn[:, g0 + j * 128:g0 + (j + 1) * 128]) for j in range(nch)], ident_b[:],
                 [b_KTn, b_ident_b], [b_pT])
            k.copy("dve", Vt[:, c0:c0 + nch, :], pT[:, 0:T_].rearrange("p (j c) -> p j c", j=nch), [b_pT], [b_Vt])
            p1, bp1 = acc_r.next()
            k.mm(fm_specs(p1, 64, wg, OFF_KR, hT, T_), [b_wg, b_hT], [bp1])
            if lat:
                p2, bp2 = acc_r.next()
                k.mm(fm_specs(p2, 64, wg, 448, hT, T_), [b_wg, b_hT], [bp2])
                t1, b_t1 = t1_r.next()
                t2, b_t2 = t2_r.next()
                k.tt("dve", t1[:, 0:T_], p1[0:64, 0:T_], cs[:, 0, 0:T_], ALU.mult, [bp1, b_cs], [b_t1])
                k.tt("dve", t2[:, 0:T_], p2[0:64, 0:T_], cs[:, 1, 0:T_], ALU.mult, [bp2, b_cs], [b_t2])
                k.tt("pool", KTr[:, g0:g0 + T_], t1[:, 0:T_], t2[:, 0:T_], ALU.add, [b_t1, b_t2], [b_KTr])
            else:
                k.copy("act", KTr[:, g0:g0 + T_], p1[0:64, 0:T_], [bp1], [b_KTr])
            if lat:
                for jq in range(2):
                    pt, bpt = acc_r.next()
                    k.mm(fm_specs(pt, 128, wg, jq * 128, hT, T_), [b_wg, b_hT], [bpt])
                    k.act(cq_sb[:, jq, 0:T_], pt[:, 0:T_], AF.Copy, [bpt], [b_cq])
                    k.act(sq[:, jq, 0:T_], pt[:, 0:T_], AF.Square, [bpt], [b_sq])
                rms_bcast([sq[:, 0, 0:T_], sq[:, 1, 0:T_]], QL, rq, b_rq, [b_sq])
                for jq in range(2):
                    k.stt("dve", cqn[:, jq, 0:T_], cq_sb[:, jq, 0:T_], wq_fm[:, jq:jq + 1], rq[:, 0:T_], ALU.mult, ALU.mult,
                          [b_cq, b_wq, b_rq], [b_cqn])
                qns = {}

                def q1(h):
                    pt, bpt = acc_r.next()
                    k.mm([(pt[:, 0:T_], wuq[:, kc, h * 192:h * 192 + 128], cqn[:, kc, 0:T_], kc == 0, kc == 1) for kc in range(2)],
                         [b_wuq, b_cqn], [bpt])
                    qn, b_qn = qn_r.next()
                    evac_copy(qn[:, 0:T_], pt[:, 0:T_], [bpt], [b_qn])
                    qns[h] = (qn, b_qn)

                def q2(h):
                    qn, b_qn = qns.pop(h)
                    pa, bpa = acc_r.next()
                    k.mm([(pa[:, 0:T_], wukT[:, h, :], qn[:, 0:T_], True, True)], [b_wukT, b_qn], [bpa])
                    evac_copy(Qn_st[:, h, 0:T_], pa[:, 0:T_], [bpa], [b_Qn_st])

                def q3(h):
                    p1, bp1 = acc_r.next()
                    k.mm([(p1[0:64, 0:T_], wuq[:, kc, h * 192 + 128:h * 192 + 192], cqn[:, kc, 0:T_], kc == 0, kc == 1) for kc in range(2)],
                         [b_wuq, b_cqn], [bp1])
                    p2, bp2 = acc_r.next()
                    k.mm([(p2[0:64, 0:T_], wuq[:, kc, H * 192 + h * 64:H * 192 + (h + 1) * 64], cqn[:, kc, 0:T_], kc == 0, kc == 1) for kc in range(2)],
                         [b_wuq, b_cqn], [bp2])
                    t1, b_t1 = t1_r.next()
                    t2, b_t2 = t2_r.next()
                    k.tt("dve", t1[:, 0:T_], p1[0:64, 0:T_], cs[:, 0, 0:T_], ALU.mult, [bp1, b_cs], [b_t1])
                    k.tt("dve", t2[:, 0:T_], p2[0:64, 0:T_], cs[:, 1, 0:T_], ALU.mult, [bp2, b_cs], [b_t2])
                    k.tt("pool", Qr_st[:, h, 0:T_], t1[:, 0:T_], t2[:, 0:T_], ALU.add, [b_t1, b_t2], [b_Qr_st])

                q1(0)
                for h in range(H):
                    if h + 1 < H:
                        q1(h + 1)
                    q3(h)
                    q2(h)
                k.dma(Qn_s[b].rearrange("h p t -> p h t")[:, :, t0:t0 + T_], Qn_st[:, :, 0:T_], reads=[b_Qn_st])
                k.dma(Qr_s[b].rearrange("h p t -> p h t")[:, :, t0:t0 + T_], Qr_st[:, :, 0:T_], reads=[b_Qr_st])
            for j in range(nch):
                pt, bpt = acc_r.next()
                k.mm([(pt[:, 0:64], hT[:, kc, j * 128:(j + 1) * 128], wdt[:, kc, :], kc == 0, kc == KC - 1) for kc in range(KC)],
                     [b_wdt, b_hT], [bpt])
                dtt, b_dtt = dtt_r.next()
                k.tt("dve", dtt[:], pt[:, 0:64], dtb_bc[:], ALU.add, [bpt, b_dtb], [b_dtt])
                k.act(dtt[:], dtt[:], AF.Exp, [b_dtt], [b_dtt])
                k.act(dtv[:, c0 + j, :], dtt[:], AF.Ln, [b_dtt], [b_dtv], bias=1.0)
            for xg in range(6):
                wg, b_wg = get_wg()
                ust, b_ust = ust_r.next()
                for cc in range(4):
                    pt, bpt = acc_r.next()
                    k.mm(fm_specs(pt, 128, wg, cc * 128, hT, T_), [b_wg, b_hT], [bpt])
                    evac_copy(ust[:, cc, 0:T_], pt[:, 0:T_], [bpt], [b_ust])
                k.dma(uT.rearrange("(c p) t -> p c t", p=128)[:, xg * 4:(xg + 1) * 4, 2 + t0:2 + t0 + T_], ust[:, :, 0:T_], reads=[b_ust])
                if xg == 2:
                    mid_hook()
            if lat:
                for zg in range(4):
                    wg, b_wg = get_wg()
                    for j in range(nch):
                        pt, bpt = acc_r.next()
                        k.mm([(pt[:], hT[:, kc, j * 128:(j + 1) * 128], wg[:, kc, :], kc == 0, kc == KC - 1) for kc in range(KC)],
                             [b_wg, b_hT], [bpt])
                        zst, b_zst = zst_r.next()
                        k.act(zst[:], pt[:], AF.Silu, [bpt], [b_zst])
                        k.dma(sz_s[b][t0 + j * 128:t0 + (j + 1) * 128, zg * 512:(zg + 1) * 512], zst[:], reads=[b_zst])
                for gg in range(4):
                    wg, b_wg = get_wg()
                    gst, b_gst = gst_r.next()
                    for cc in range(4):
                        pt, bpt = acc_r.next()
                        k.mm(fm_specs(pt, 128, wg, cc * 128, hT, T_), [b_wg, b_hT], [bpt])
                        k.act(gst[:, cc, 0:T_], pt[:, 0:T_], AF.Sigmoid, [bpt], [b_gst])
                    k.dma(gT_s[b].rearrange("(c p) t -> p c t", p=128)[:, gg * 4:(gg + 1) * 4, t0:t0 + T_], gst[:, :, 0:T_], reads=[b_gst])
        tl = tiles_of()
        gspecs = []
        for (kind_, _i0, _n) in tl:
            gspecs += [[(0, 448, 0), (448, 64, OFF_KRS)]] + [[(0, 512, OFF_X + xg * 512)] for xg in range(6)]
            if kind_ == "lat":
                gspecs += [[(0, 512, OFF_Z + zg * 512)] for zg in range(4)] + [[(0, 512, OFF_G + gg * 512)] for gg in range(4)]
        issued = []
        gi = [0]

        def get_wg():
            idx = gi[0]
            gi[0] += 1
            while len(issued) < min(len(gspecs), idx + 3):
                w_, bw_ = wg_r.next()
                for (d0, n_, s0) in gspecs[len(issued)]:
                    k.dma(w_[:, :, d0:d0 + n_], w_in_bf[:, :, s0:s0 + n_], reads=[wdep[s0]], writes=[bw_])
                issued.append((w_, bw_))
            return issued[idx]

        cxs = [None] * len(tl)
        cxs[0] = stage1(*tl[0])
        stage1b(cxs[0])
        for ti in range(len(tl)):
            if ti + 1 < len(tl):
                cxs[ti + 1] = stage1(*tl[ti + 1])
                stage2(cxs[ti], lambda c_=cxs[ti + 1]: stage1b(c_))
            else:
                stage2(cxs[ti], lambda: None)
        k.dma(KTn_s[b], KTn[:], reads=[b_KTn])
        k.dma(KTr_s[b], KTr[:], reads=[b_KTr])
        k.dma(Vt_s[b], Vt[:], reads=[b_Vt])
        P.barrier()
        ph.close()

    def phase_C(b):
        ph = contextlib.ExitStack()
        convw, b_convw = sb("C_convw", [128, 24, CW], F32, ph)
        diag, b_diag = sb("C_diag", [128, 24, CW, 128], BF16, ph)
        uw_r = ring("C_uw", 2, [128, 24, 516], BF16, ph)
        st_r = ring("C_st", 4, [128, 512], BF16, ph)
        acc_r = ring("C_acc", 6, [128, 512], F32, ph, psum=True)
        k.dma(convw[:], convw_fm_d, writes=[b_convw])
        k.tt("pool", diag[:], ident_f[:].unsqueeze(1).unsqueeze(1).to_broadcast([128, 24, CW, 128]),
             convw[:].unsqueeze(3).to_broadcast([128, 24, CW, 128]), ALU.mult, [b_ident_f, b_convw], [b_diag])
        for (kind, i0, nch) in tiles_of():
            lat = kind == "lat"
            T_ = nch * 128
            t0 = i0 * 128 if lat else 0
            g0 = (2 + i0) * 128 if lat else 0
            uT = uTl[b] if lat else uTc[b]
            uw, b_uw = uw_r.next()
            k.dma(uw[:, :, 0:T_ + 4], uT.rearrange("(c p) t -> p c t", p=128)[:, :, t0:t0 + T_ + 4], writes=[b_uw])
            for j in range(nch):
                for grp in range(5):
                    ch0 = grp * 4
                    pt, bpt = acc_r.next()
                    specs = [(pt[:], ones_b[0:1, 0:128], convb_rb[0:1, ch0 * 128:ch0 * 128 + 512], True, False)]
                    for cc in range(4):
                        for tap in range(CW):
                            specs.append((pt[:, cc * 128:(cc + 1) * 128], uw[:, ch0 + cc, j * 128 + tap:j * 128 + tap + 128],
                                          diag[:, ch0 + cc, tap, :], False, tap == CW - 1))
                    k.mm(specs, [b_ones_b, b_convb_rb, b_uw, b_diag], [bpt])
                    stt_, b_stt = st_r.next()
                    k.act(stt_[:], pt[:], AF.Silu, [bpt], [b_stt])
                    if grp < 4:
                        k.dma(xs_s[b][g0 + j * 128:g0 + (j + 1) * 128, grp * 512:(grp + 1) * 512], stt_[:], reads=[b_stt])
                    else:
                        k.dma(Bt_s[b][g0 + j * 128:g0 + (j + 1) * 128, :], stt_[:], reads=[b_stt])
            for ch in range(16, 24):
                pt, bpt = acc_r.next()
                k.mm([(pt[:, 0:T_], diag[:, ch, tap, :], uw[:, ch, tap:tap + T_], tap == 0, tap == CW - 1) for tap in range(CW)],
                     [b_uw, b_diag], [bpt])
                stt_, b_stt = st_r.next()
                k.act(stt_[:, 0:T_], pt[:, 0:T_], AF.Silu, [bpt, b_convb_fm], [b_stt], bias=convb_fm[:, ch:ch + 1])
                k.dma(BCf_s[b][(ch - 16) * 128:(ch - 15) * 128, g0:g0 + T_], stt_[:, 0:T_], reads=[b_stt])
        P.barrier()
        ph.close()

    def phase_S(b, d):
        fwd = d == 0
        ph = contextlib.ExitStack()
        iu, im = (0, 1) if fwd else (2, 3)
        U = tri_f[:, iu, :]
        Ub = tri_b[:, iu, :]
        TMk = tri_f[:, im, :]
        hs = slice(d * 32, (d + 1) * 32)
        S32, _ = sb("S_S32", [128, DI], F32, ph)
        Sbf, _ = sb("S_Sbf", [128, DI], BF16, ph)
        bS32 = [Buf("S32_%d" % g) for g in range(4)]
        bSbf = [Buf("Sbf_%d" % g) for g in range(4)]
        xs_r = ring("S_xs", 3, [128, DI], BF16, ph)
        Bt_r = ring("S_Bt", 2, [128, 512], BF16, ph)
        BC_r = ring("S_BC", 2, [128, 8, 128], BF16, ph)
        sm_r = ring("S_sm", 3, [128, 8, 32], F32, ph)
        adb_r = ring("S_adb", 3, [128, 32], BF16, ph)
        xd_r = ring("S_xd", 2, [128, DI], BF16, ph)
        xdd_r = ring("S_xdd", 2, [128, DI], BF16, ph)
        Gm_r = ring("S_Gm", 2, [128, 4, 128], BF16, ph)
        R_r = ring("S_R", 2, [128, 2, 8, 128], BF16, ph)
        E_r = ring("S_E", 2, [128, 8, 128], BF16, ph)
        W_r = ring("S_W", 2, [128, 4, 8, 128], BF16, ph)
        tO_r = ring("S_tO", 2, [128, 512], F32, ph)
        ys_r = ring("S_ys", 2, [128, DI], F32 if fwd else BF16, ph)
        psm_r = ring("S_psm", 1, [128, 512], F32, ph, psum=True)
        pG_r = ring("S_pG", 1, [128, 512], F32, ph, psum=True)
        pa_r = ring("S_pa", 2, [128, 512], F32, ph, psum=True)
        pb_r = ring("S_pb", 3, [128, 512], F32, ph, psum=True)
        if fwd:
            DIh, b_DIh = sb("S_DIh", [128, 32, 128], BF16, ph)
            yb_r = ring("S_yb", 2, [128, DI], BF16, ph)
            sz_r = ring("S_sz", 3, [128, DI], BF16, ph)
            yg, b_yg = sb("S_yg", [128, DI], F32, ph)
            ygn, b_ygn = sb("S_ygn", [128, DI], BF16, ph)
            junk, b_junk = sb("S_junk", [128, DI], BF16, ph)
            st4, b_st4 = sb("S_st4", [128, 4], F32, ph)
            wssd_bc, b_wssd = sb("S_wssd", [128, DI], F32, ph)
            gst, b_gst = sb("S_gst", [128, 16, 512], BF16, ph)
            pT_r = ring("S_pT", 1, [128, 1024], BF16, ph, psum=True)
            k.dma(wssd_bc[:], wssd_row_d.partition_broadcast(128), writes=[b_wssd])
            k.tt("dve", DIh[:], ident_f[:].unsqueeze(1).to_broadcast([128, 32, 128]), dsk_bc[:].unsqueeze(2).to_broadcast([128, 32, 128]),
                 ALU.mult, [b_ident_f, b_dsk], [b_DIh])
        k.memset("dve", S32[:], 0.0, bS32)
        k.memset("dve", Sbf[:], 0.0, bSbf)
        order = list(range(NCH)) if fwd else [1, 0] + list(range(NCH - 1, 1, -1))

        def prep_head(ci):
            c = order[ci]
            cx = dict(c=c, lat=c >= 2, last=ci == len(order) - 1, g0=c * 128, t0=(c - 2) * 128, late=[])
            lat, last, g0, t0 = cx["lat"], cx["last"], cx["g0"], cx["t0"]
            xs, b_xs = xs_r.next()
            k.dma(xs[:], xs_s[b][g0:g0 + 128, :], writes=[b_xs])
            cx.update(xs=xs, b_xs=b_xs)
            if not last:
                Bt, b_Bt = Bt_r.next()
                k.dma(Bt[:], Bt_s[b][g0:g0 + 128, :], writes=[b_Bt])
                cx.update(Bt=Bt, b_Bt=b_Bt)
            if lat:
                BC, b_BC = BC_r.next()
                k.dma(BC[:], BCf_s[b].rearrange("(c p) t -> p c t", p=128)[:, :, g0:g0 + 128], writes=[b_BC])
                cx.update(BC=BC, b_BC=b_BC)
                if fwd:
                    ybt, b_ybt = yb_r.next()
                    k.dma(ybt[:], yb_s[b][t0:t0 + 128, :], writes=[b_ybt])
                    szt, b_szt = sz_r.next()
                    k.dma(szt[:], sz_s[b][t0:t0 + 128, :], writes=[b_szt])
                    cx.update(ybt=ybt, b_ybt=b_ybt, szt=szt, b_szt=b_szt)
            sm, b_sm = sm_r.next()
            ad, acs, EA, cd, tmp, wS = (sm[:, i, :] for i in range(6))
            cx.update(sm=sm, b_sm=b_sm)
            dtc = dtv[:, c, hs]
            k.tt("dve", ad, dtc, a_bc[:, hs], ALU.mult, [b_dtv, b_abc], [b_sm])
            adb, b_adb = adb_r.next()
            k.copy("dve", adb[:], ad, [b_sm], [b_adb])
            k.copy("dve", sm[:, 6, :], adb[:], [b_adb], [b_sm])
            k.tt("dve", sm[:, 7, :], ad, sm[:, 6, :], ALU.subtract, [b_sm], [b_sm])
            psm, b_psm = psm_r.next()
            k.mm([(psm[:, 0:32], TMk, ad, True, True), (psm[:, 32:64], ones_f[:], ad, True, True)], [b_tri, b_ones_f, b_sm], [b_psm])
            k.act(acs, psm[:, 0:32], AF.Copy, [b_psm], [b_sm])
            k.act(EA, psm[:, 0:32], AF.Exp, [b_psm], [b_sm])
            k.act(cd, psm[:, 32:64], AF.Exp, [b_psm], [b_sm])
            k.tt("dve", tmp, psm[:, 32:64], acs, ALU.subtract, [b_psm, b_sm], [b_sm])
            k.act(tmp, tmp, AF.Exp, [b_sm], [b_sm])
            k.tt("dve", wS, tmp, dtc, ALU.mult, [b_sm, b_dtv], [b_sm])
            xs3 = xs[:].rearrange("p (h q) -> p h q", h=32)
            if lat:
                pG, b_pG = pG_r.next()
                k.mm([(pG[:, g * 128:(g + 1) * 128], BC[:, g, :], BC[:, 4 + g, :], True, True) for g in range(4)], [b_BC], [b_pG])
                Gm, b_Gm = Gm_r.next()
                k.tt("dve", Gm[:], pG[:].rearrange("p (g l) -> p g l", g=4), TMk.unsqueeze(1).to_broadcast([128, 4, 128]), ALU.mult,
                     [b_pG, b_tri], [b_Gm])
                W4, b_W4 = W_r.next()
                xd, b_xd = xd_r.next()
                cx.update(W4=W4, b_W4=b_W4, xd=xd, b_xd=b_xd, Gm=Gm, b_Gm=b_Gm, Rs={})
                cx["late"].append(lambda: k.tt("pool", xd[:].rearrange("p (h q) -> p h q", h=32), xs3,
                                               dtc.unsqueeze(2).to_broadcast([128, 32, 64]), ALU.mult, [b_xs, b_dtv], [b_xd]))
            if not last:
                xdd, b_xdd = xdd_r.next()
                cx.update(xdd=xdd, b_xdd=b_xdd)
                cx["late"].append(lambda: k.tt("pool", xdd[:].rearrange("p (h q) -> p h q", h=32), xs3,
                                               wS.unsqueeze(2).to_broadcast([128, 32, 64]), ALU.mult, [b_xs, b_sm], [b_xdd]))
            return cx

        def prep_R(cx, g):
            sm, b_sm = cx["sm"], cx["b_sm"]
            ad = sm[:, 0, :]
            Rg, b_Rg = R_r.next()
            for part in range(2):
                adp = sm[:, 6 + part, :]
                if g % 2 == 1:
                    k.tt("pool", Rg[:, part, :, :], adp[:, g * 8:(g + 1) * 8].unsqueeze(2).to_broadcast([128, 8, 128]),
                         TMk.unsqueeze(1).to_broadcast([128, 8, 128]), ALU.mult, [b_sm, b_tri], [b_Rg])
                else:
                    for r in range(8):
                        k.act(Rg[:, part, r, :], TMk, AF.Identity, [b_sm, b_tri], [b_Rg], scale=adp[:, g * 8 + r:g * 8 + r + 1])
            cx["Rs"][g] = (Rg, b_Rg)

        def prep_G(cx, g):
            Rg, b_Rg = cx["Rs"].pop(g)
            W4, b_W4, Gm, b_Gm = cx["W4"], cx["b_W4"], cx["Gm"], cx["b_Gm"]
            Eg, b_Eg = E_r.next()
            for hf in range(2):
                pa, b_pa = pa_r.next()
                k.mm([(pa[:], Ub, Rg[:, 0, hf * 4:(hf + 1) * 4, :], True, False),
                      (pa[:], Ub, Rg[:, 1, hf * 4:(hf + 1) * 4, :], False, True)], [b_tri_b, b_Rg], [b_pa])
                k.act(Eg[:, hf * 4:(hf + 1) * 4, :], pa[:].rearrange("p (r l) -> p r l", r=4), AF.Exp, [b_pa], [b_Eg])
            k.tt("dve", W4[:, g, :, :], Eg[:], Gm[:, g, :].unsqueeze(1).to_broadcast([128, 8, 128]), ALU.mult, [b_Eg, b_Gm], [b_W4])

        def main_G(cx, g):
            c, lat, last, t0 = cx["c"], cx["lat"], cx["last"], cx["t0"]
            sm, b_sm = cx["sm"], cx["b_sm"]
            ad, acs, EA, cd, tmp, wS = (sm[:, i, :] for i in range(6))
            gs = slice(g * 512, (g + 1) * 512)
            hg = slice(g * 8, (g + 1) * 8)
            if lat:
                BC, b_BC, W4, b_W4, xd, b_xd = cx["BC"], cx["b_BC"], cx["W4"], cx["b_W4"], cx["xd"], cx["b_xd"]
                if g == 0:
                    cx["ys"], cx["b_ys"] = ys_r.next()
                ys, b_ys = cx["ys"], cx["b_ys"]
                pY, b_pY = pb_r.next()
                specs = []
                rd_ = [b_ident_b, b_W4, b_xd]
                if fwd:
                    specs.append((pY[:], ident_b[:], cx["ybt"][:, gs], True, False))
                    rd_ += [cx["b_ybt"], b_DIh, cx["b_xs"]]
                for r in range(8):
                    hh = g * 8 + r
                    if fwd:
                        specs.append((pY[:, r * 64:(r + 1) * 64], DIh[:, hh, :], cx["xs"][:, hh * 64:(hh + 1) * 64], False, False))
                    specs.append((pY[:, r * 64:(r + 1) * 64], W4[:, g, r, :], xd[:, hh * 64:(hh + 1) * 64], not fwd, True))
                k.mm(specs, rd_, [b_pY])
                pO, b_pO = pb_r.next()
                k.mm([(pO[:], BC[:, 4 + g, :], Sbf[:, gs], True, True)], [b_BC, bSbf[g]], [b_pO])
                tO, b_tO = tO_r.next()
                k.tt("dve", tO[:].rearrange("p (r q) -> p r q", r=8), pO[:].rearrange("p (r q) -> p r q", r=8),
                     EA[:, hg].unsqueeze(2).to_broadcast([128, 8, 64]), ALU.mult, [b_pO, b_sm], [b_tO])
                k.tt("dve", ys[:, gs], pY[:], tO[:], ALU.add, [b_pY, b_tO], [b_ys])
            if not last:
                Bt, b_Bt, xdd, b_xdd = cx["Bt"], cx["b_Bt"], cx["xdd"], cx["b_xdd"]
                pS, b_pS = pb_r.next()
                k.mm([(pS[:], Bt[:, g * 128:(g + 1) * 128], xdd[:, gs], True, True)], [b_Bt, b_xdd], [b_pS])
                k.tt("pool", S32[:, gs].rearrange("p (r q) -> p r q", r=8), S32[:, gs].rearrange("p (r q) -> p r q", r=8),
                     cd[:, hg].unsqueeze(2).to_broadcast([128, 8, 64]), ALU.mult, [bS32[g], b_sm], [bS32[g]])
                k.tt("dve", S32[:, gs], pS[:], S32[:, gs], ALU.add, [b_pS, bS32[g]], [bS32[g]])
                k.copy("act", Sbf[:, gs], S32[:, gs], [bS32[g]], [bSbf[g]])

        def post_parts(cx):
            c, t0 = cx["c"], cx["t0"]
            ys, b_ys, szt, b_szt = cx["ys"], cx["b_ys"], cx["szt"], cx["b_szt"]
            j = (c - 2) % 4

            def p1():
                k.tt("dve", yg[:], ys[:], szt[:], ALU.mult, [b_ys, b_szt], [b_yg])
                k.memset("dve", st4[:], 0.0, [b_st4])
                k.act(junk[:], yg[:], AF.Square, [b_yg], [b_junk, b_st4], accum=st4[:, 0:1])
                rstd_op(st4[:, 2:3], st4[:, 0:1], DI, 1, [b_st4], [b_st4])

            def p2():
                k.stt("dve", ygn[:], yg[:], st4[:, 2:3], wssd_bc[:], ALU.mult, ALU.mult, [b_yg, b_st4, b_wssd], [b_ygn])

            def p3():
                for hf in range(2):
                    pT, b_pT = pT_r.next()
                    k.tr([(pT[:, i * 128:(i + 1) * 128], ygn[:, (hf * 8 + i) * 128:(hf * 8 + i + 1) * 128]) for i in range(8)], ident_b[:],
                         [b_ygn, b_ident_b], [b_pT])
                    k.copy("act", gst[:, hf * 8:(hf + 1) * 8, j * 128:(j + 1) * 128], pT[:].rearrange("p (i t) -> p i t", i=8), [b_pT], [b_gst])
                if j == 3:
                    tt0 = t0 - 384
                    k.dma(ygT_s[b].rearrange("(c p) t -> p c t", p=128)[:, :, tt0:tt0 + 512], gst[:], reads=[b_gst])
            return [p1, p2, p3]

        n_o = len(order)
        cur = prep_head(0)
        if cur["lat"]:
            prep_R(cur, 0)
            for g in range(4):
                if g < 3:
                    prep_R(cur, g + 1)
                prep_G(cur, g)
        for fn in cur["late"]:
            fn()
        pend = []
        for ci in range(n_o):
            nxt = prep_head(ci + 1) if ci + 1 < n_o else None
            nl = nxt is not None and nxt["lat"]
            if nl:
                prep_R(nxt, 0)
            for g in range(4):
                if nl:
                    if g < 3:
                        prep_R(nxt, g + 1)
                    prep_G(nxt, g)
                main_G(cur, g)
                if pend:
                    pend.pop(0)()
                if nxt is not None and g in (1, 3) and nxt["late"]:
                    nxt["late"].pop(0)()
            while nxt is not None and nxt["late"]:
                nxt["late"].pop(0)()
            while pend:
                pend.pop(0)()
            if cur["lat"] and not fwd:
                k.dma(yb_s[b][cur["t0"]:cur["t0"] + 128, :], cur["ys"][:], reads=[cur["b_ys"]])
            if cur["lat"] and fwd:
                pend = post_parts(cur)
            cur = nxt
        while pend:
            pend.pop(0)()
        P.barrier()
        ph.close()

    def phase_Q(b):
        ph = contextlib.ExitStack()
        KTn, b_KTn = sb("Q_KTn", [128, NT], BF16, ph)
        KTr, b_KTr = sb("Q_KTr", [128, NT], BF16, ph)
        Vt, b_Vt = sb("Q_Vt", [128, NCH, 128], BF16, ph)
        wuv, b_wuv = sb("Q_wuv", [128, H, DV], BF16, ph)
        Qn_r = ring("Q_Qn", 2, [128, H, 512], BF16, ph)
        Qr_r = ring("Q_Qr", 2, [128, H, 512], BF16, ph)
        PT_r = ring("Q_PT", 6, [128, 512], BF16, ph)
        aD_r = ring("Q_aD", 2, [128, 512], F32, ph)
        PS_r = ring("Q_PS", 2, [128, 512], BF16, ph)
        rd_r = ring("Q_rd", 2, [128, 512], F32, ph)
        On_r = ring("Q_On", 2, [128, 512], BF16, ph)
        ym_r = ring("Q_ym", 2, [128, H, 512], BF16, ph)
        pS_r = ring("Q_pS", 4, [128, 512], F32, ph, psum=True)
        pO_r = ring("Q_pO", 2, [128, 512], F32, ph, psum=True)
        pD_r = ring("Q_pD", 1, [128, 512], F32, ph, psum=True)
        pU_r = ring("Q_pU", 1, [128, 512], F32, ph, psum=True)
        k.dma(KTn[:], KTn_s[b], writes=[b_KTn])
        k.dma(KTr[0:64, :], KTr_s[b], writes=[b_KTr])
        k.dma(KTr[64:128, :], KTr_s[b], writes=[b_KTr])
        k.dma(Vt[:], Vt_s[b], writes=[b_Vt])
        k.dma(wuv[:], wb["w_uv"][0], writes=[b_wuv])
        NP = NCH // 2

        def load_q(ti):
            t0 = ti * 512
            Qn, b_Qn = Qn_r.next()
            Qr, b_Qr = Qr_r.next()
            k.dma(Qn[:], Qn_s[b].rearrange("h p t -> p h t")[:, :, t0:t0 + 512], writes=[b_Qn])
            k.dma(Qr[0:64, :, :], Qr_s[b].rearrange("h p t -> p h t")[:, :, t0:t0 + 512], writes=[b_Qr])
            k.dma(Qr[64:128, :, :], Qr_s[b].rearrange("h p t -> p h t")[:, :, t0:t0 + 512], writes=[b_Qr])
            return Qn, b_Qn, Qr, b_Qr

        qnext = load_q(0)
        for ti in range(S // 512):
            t0 = ti * 512
            Qn, b_Qn, Qr, b_Qr = qnext
            if ti + 1 < S // 512:
                qnext = load_q(ti + 1)
            ym, b_ym = ym_r.next()
            steps = [(h, kp) for h in range(H) for kp in range(NP)]
            pSs = {}
            tails = []

            def emit_S(i):
                h, kp = steps[i]
                k0, k1 = 2 * kp, 2 * kp + 1
                pA, b_pA = pS_r.next()
                pB, b_pB = pS_r.next()
                k.mm([(pA[:], KTn[:, k0 * 128:(k0 + 1) * 128], Qn[:, h, :], True, False),
                      (pB[:], KTn[:, k1 * 128:(k1 + 1) * 128], Qn[:, h, :], True, False),
                      (pA[:], KTr[0:64, k0 * 128:(k0 + 1) * 128], Qr[0:64, h, :], False, True),
                      (pB[:], KTr[64:128, k1 * 128:(k1 + 1) * 128], Qr[64:128, h, :], False, True)],
                     [b_KTn, b_KTr, b_Qn, b_Qr], [b_pA, b_pB])
                pSs[i] = ((pA, b_pA), (pB, b_pB))

            emit_S(0)
            for i, (h, kp) in enumerate(steps):
                if i + 1 < len(steps):
                    emit_S(i + 1)
                (pA, b_pA), (pB, b_pB) = pSs.pop(i)
                PA, b_PA = PT_r.next()
                PB, b_PB = PT_r.next()
                k.act(PA[:], pA[:], AF.Exp, [b_pA], [b_PA], scale=ATTN_SCALE)
                k.act(PB[:], pB[:], AF.Exp, [b_pB], [b_PB], scale=ATTN_SCALE)
                PS, b_PS = PS_r.next()
                k.tt("dve", PS[:], PA[:], PB[:], ALU.add, [b_PA, b_PB], [b_PS])
                if kp == 0:
                    pO, b_pO = pO_r.next()
                    aD, b_aD = aD_r.next()
                    k.copy("dve", aD[:], PS[:], [b_PS], [b_aD])
                else:
                    k.tt("dve", aD[:], aD[:], PS[:], ALU.add, [b_aD, b_PS], [b_aD])
                k.mm([(pO[:], Vt[:, 2 * kp, :], PA[:], kp == 0, False),
                      (pO[:], Vt[:, 2 * kp + 1, :], PB[:], False, kp == NP - 1)], [b_Vt, b_PA, b_PB], [b_pO])
                if kp == NP - 1:
                    def tail1(h=h, pO=pO, b_pO=b_pO, aD=aD, b_aD=b_aD, due=i + 7):
                        pD, b_pD = pD_r.next()
                        k.mm([(pD[:], ones_f[:], aD[:], True, True)], [b_ones_f, b_aD], [b_pD])
                        rd, b_rd = rd_r.next()
                        k.recip(rd[:], pD[:], [b_pD], [b_rd])
                        On, b_On = On_r.next()
                        k.tt("dve", On[:], pO[:], rd[:], ALU.mult, [b_pO, b_rd], [b_On])

                        def tail2():
                            pU, b_pU = pU_r.next()
                            k.mm([(pU[:], wuv[:, h, :], On[:], True, True)], [b_wuv, b_On], [b_pU])
                            k.copy("act", ym[:, h, :], pU[:], [b_pU], [b_ym])
                        tails.append([due, 1, tail2])
                    tails.append([i + 2, 0, tail1])
                for tl_ in list(tails):
                    if tl_[0] <= i:
                        tails.remove(tl_)
                        tl_[2]()
            while tails:
                tl_ = tails.pop(0)
                tl_[2]()
            k.dma(ymT_s[b].rearrange("(h p) t -> p h t", p=128)[:, :, t0:t0 + 512], ym[:], reads=[b_ym])
        P.barrier()
        ph.close()

    def phase_T1(b):
        ph = contextlib.ExitStack()
        womla, b_womla = sb("T_womla", [128, KC, D], BF16, ph)
        wossd, b_wossd = sb("T_wossd", [128, 16, D], BF16, ph)
        wout, b_wout = sb("T_wout", [128, KC, D], BF16, ph)
        g1bc, b_g1bc = sb("T_g1bc", [128, D], F32, ph)
        ym_r = ring("T_ym", 2, [128, KC, 512], BF16, ph)
        yg_r = ring("T_yg", 1, [128, 16, 512], BF16, ph)
        gt_r = ring("T_gt", 1, [128, 16, 512], BF16, ph)
        xt_r = ring("T_xt", 2, [128, 4, D], F32, ph)
        mg_r = ring("T_mg", 2, [128, KC, 512], BF16, ph)
        m1_r = ring("T_m1", 2, [128, 512], F32, ph)
        m2_r = ring("T_m2", 2, [128, 512], F32, ph)
        tx_r = ring("T_tx", 2, [128, 512], F32, ph)
        acc_r = ring("T_acc", 6, [128, 512], F32, ph, psum=True)
        k.dma(womla[:], wb["w_o_mla"][0], writes=[b_womla])
        k.dma(wossd[:], wb["w_o_ssd"][0], writes=[b_wossd])
        k.dma(wout[:], wb["w_out"][0], writes=[b_wout])
        k.dma(g1bc[:], gb_s[0, b:b + 1, :].partition_broadcast(128), writes=[b_g1bc])
        def load_act(ti):
            t0 = ti * 512
            ym, b_ym = ym_r.next()
            yg, b_yg = yg_r.next()
            gt, b_gt = gt_r.next()
            k.dma(ym[:], ymT_s[b].rearrange("(c p) t -> p c t", p=128)[:, :, t0:t0 + 512], writes=[b_ym])
            k.dma(yg[:], ygT_s[b].rearrange("(c p) t -> p c t", p=128)[:, :, t0:t0 + 512], writes=[b_yg])
            k.dma(gt[:], gT_s[b].rearrange("(c p) t -> p c t", p=128)[:, :, t0:t0 + 512], writes=[b_gt])
            return ym, b_ym, yg, b_yg, gt, b_gt

        def load_x(ti):
            t0 = ti * 512
            xt, b_xt = xt_r.next()
            k.dma(xt[:], x_d[b][t0:t0 + 512, :].rearrange("(j p) d -> p j d", p=128), writes=[b_xt])
            return xt, b_xt

        nxt_act = load_act(0)
        nxt_x = load_x(0)
        for ti in range(S // 512):
            t0 = ti * 512
            ym, b_ym, yg, b_yg, gt, b_gt = nxt_act
            xt, b_xt = nxt_x
            mg, b_mg = mg_r.next()
            if ti + 1 < S // 512:
                nxt_x = load_x(ti + 1)
            for n_ in range(8):
                ns = slice(n_ * 128, (n_ + 1) * 128)
                pM, b_pM = acc_r.next()
                k.mm([(pM[:], womla[:, kc, ns], ym[:, kc, :], kc == 0, kc == KC - 1) for kc in range(KC)], [b_womla, b_ym], [b_pM])
                m1, b_m1 = m1_r.next()
                k.tt("dve", m1[:], pM[:], gt[:, n_, :], ALU.mult, [b_pM, b_gt], [b_m1])
                pS, b_pS = acc_r.next()
                k.mm([(pS[:], wossd[:, kc, ns], yg[:, kc, :], kc == 0, kc == 15) for kc in range(16)], [b_wossd, b_yg], [b_pS])
                m2, b_m2 = m2_r.next()
                k.tt("dve", m2[:], pS[:], gt[:, 8 + n_, :], ALU.mult, [b_pS, b_gt], [b_m2])
                k.tt("pool", mg[:, n_, :], m1[:], m2[:], ALU.add, [b_m1, b_m2], [b_mg])
            if ti + 1 < S // 512:
                nxt_act = load_act(ti + 1)
            for j in range(4):
                for nh in range(2):
                    cs_ = slice(nh * 512, (nh + 1) * 512)
                    pX, b_pX = acc_r.next()
                    k.mm([(pX[:], mg[:, kc, j * 128:(j + 1) * 128], wout[:, kc, cs_], kc == 0, kc == KC - 1) for kc in range(KC)],
                         [b_wout, b_mg], [b_pX])
                    tx, b_tx = tx_r.next()
                    k.tt("dve", tx[:], pX[:], g1bc[:, cs_], ALU.mult, [b_pX, b_g1bc], [b_tx])
                    k.tt("pool", xt[:, j, cs_], tx[:], xt[:, j, cs_], ALU.add, [b_tx, b_xt], [b_xt])
            k.dma(x1_s[b][t0:t0 + 512, :].rearrange("(j p) d -> p j d", p=128), xt[:], reads=[b_xt])
        P.barrier()
        ph.close()

    def phase_T2(b):
        ph = contextlib.ExitStack()
        wdn, b_wdn = sb("F_wdn", [128, NFF, D], BF16, ph)
        g2bc, b_g2bc = sb("F_g2bc", [128, D], F32, ph)
        wfin, b_wfin = sb("F_wfin", [128, D], F32, ph)
        xt_r = ring("F_xt", 2, [128, 4, D], F32, ph)
        xn_r = ring("F_xn", 1, [128, 4, D], BF16, ph)
        junk, b_junk = sb("F_junk", [128, D], BF16, ph)
        st_r = ring("F_st", 2, [128, 12], F32, ph)
        hT_r = ring("F_hT", 2, [128, KC, 512], BF16, ph)
        wf_r = ring("F_wf", 3, [128, KC, 512], BF16, ph)
        sg_r = ring("F_sg", 2, [128, 512], F32, ph)
        aT, b_aT = sb("F_aT", [128, NFF, 512], BF16, ph)
        tx_r = ring("F_tx", 2, [128, 512], F32, ph)
        x2_r = ring("F_x2", 1, [128, D], F32, ph)
        ot_r = ring("F_ot", 1, [128, D], F32, ph)
        fs_r = ring("F_fs", 2, [128, 4], F32, ph)
        acc_r = ring("F_acc", 6, [128, 512], F32, ph, psum=True)
        pT_r = ring("F_pT", 2, [128, 512], BF16, ph, psum=True)
        wfi = wb["w_ffn_in"][0]
        k.dma(wdn[:], wb["w_down"][0], writes=[b_wdn])
        k.dma(g2bc[:], gb_s[1, b:b + 1, :].partition_broadcast(128), writes=[b_g2bc])
        k.dma(wfin[:], wfinal_row_d.partition_broadcast(128), writes=[b_wfin])

        def stage1(ti):
            t0 = ti * 512
            xt, b_xt = xt_r.next()
            xn, b_xn = xn_r.next()
            st, b_st = st_r.next()
            hT, b_hT = hT_r.next()
            k.dma(xt[:], x1_s[b][t0:t0 + 512, :].rearrange("(j p) d -> p j d", p=128), writes=[b_xt])
            k.memset("dve", st[:], 0.0, [b_st])
            for j in range(4):
                k.act(junk[:], xt[:, j, :], AF.Square, [b_xt], [b_junk, b_st], accum=st[:, j:j + 1])
                rstd_op(st[:, 8 + j:9 + j], st[:, j:j + 1], D, 1, [b_st], [b_st])
                k.tt("pool", xn[:, j, :], xt[:, j, :], st[:, 8 + j:9 + j].to_broadcast([128, D]), ALU.mult, [b_xt, b_st], [b_xn])
            return dict(t0=t0, xt=xt, b_xt=b_xt, hT=hT, b_hT=b_hT, xn=xn, b_xn=b_xn)

        def stage1b(cx):
            hT = cx["hT"]; b_hT = cx["b_hT"]; xn = cx["xn"]; b_xn = cx["b_xn"]
            for kc in range(KC):
                pT, b_pT = pT_r.next()
                k.tr([(pT[:, j * 128:(j + 1) * 128], xn[:, j, kc * 128:(kc + 1) * 128]) for j in range(4)], ident_b[:],
                     [b_xn, b_ident_b], [b_pT])
                if kc % 2 == 0:
                    k.act(hT[:, kc, :], pT[:], AF.Identity, [b_pT, b_sc2e, b_modT], [b_hT],
                          scale=sc2e[:, kc, b:b + 1], bias=modT[:, 24 + kc, b:b + 1])
                else:
                    k.ts(hT[:, kc, :], pT[:], sc2e[:, kc, b:b + 1], modT[:, 24 + kc, b:b + 1], ALU.mult, ALU.add,
                         [b_pT, b_sc2e, b_modT], [b_hT])

        def stage2(cx, mid_hook):
            t0 = cx["t0"]; xt = cx["xt"]; b_xt = cx["b_xt"]; hT = cx["hT"]; b_hT = cx["b_hT"]
            for fg in range(11):
                wf, b_wf = get_wf()
                for i in range(2):
                    pG, b_pG = acc_r.next()
                    k.mm([(pG[:], wf[:, kc, i * 128:(i + 1) * 128], hT[:, kc, :], kc == 0, kc == KC - 1) for kc in range(KC)], [b_wf, b_hT], [b_pG])
                    pU, b_pU = acc_r.next()
                    k.mm([(pU[:], wf[:, kc, 256 + i * 128:256 + (i + 1) * 128], hT[:, kc, :], kc == 0, kc == KC - 1) for kc in range(KC)], [b_wf, b_hT], [b_pU])
                    sg, b_sg = sg_r.next()
                    k.act(sg[:], pG[:], AF.Silu, [b_pG], [b_sg])
                    k.tt("dve", aT[:, 2 * fg + i, :], pU[:], sg[:], ALU.mult, [b_pU, b_sg], [b_aT])
                if fg == 7:
                    mid_hook()
            fs, b_fs = fs_r.next()
            k.memset("dve", fs[:], 0.0, [b_fs])
            for j in range(4):
                x2, b_x2 = x2_r.next()
                for nh in range(2):
                    cs_ = slice(nh * 512, (nh + 1) * 512)
                    pD, b_pD = acc_r.next()
                    k.mm([(pD[:], aT[:, i, j * 128:(j + 1) * 128], wdn[:, i, cs_], i == 0, i == NFF - 1) for i in range(NFF)], [b_aT, b_wdn], [b_pD])
                    tx, b_tx = tx_r.next()
                    k.tt("dve", tx[:], pD[:], g2bc[:, cs_], ALU.mult, [b_pD, b_g2bc], [b_tx])
                    k.tt("pool", x2[:, cs_], tx[:], xt[:, j, cs_], ALU.add, [b_tx, b_xt], [b_x2])
                k.act(junk[:], x2[:], AF.Square, [b_x2], [b_junk, b_fs], accum=fs[:, 0:1])
                rstd_op(fs[:, 2:3], fs[:, 0:1], D, 1, [b_fs], [b_fs])
                ot, b_ot = ot_r.next()
                k.stt("dve", ot[:], x2[:], fs[:, 2:3], wfin[:], ALU.mult, ALU.mult, [b_x2, b_fs, b_wfin], [b_ot])
                P.out_dmas.append(k.dma(out_d[b][t0 + j * 128:t0 + (j + 1) * 128, :], ot[:], reads=[b_ot]))
                if j < 3:
                    fs, b_fs = fs_r.next()
                    k.memset("dve", fs[:], 0.0, [b_fs])

        n_t = S // 512
        issued = []
        gi = [0]

        def get_wf():
            idx = gi[0]
            gi[0] += 1
            while len(issued) < min(11 * n_t, idx + 3):
                fg = len(issued) % 11
                w_, bw_ = wf_r.next()
                k.dma(w_[:, :, 0:256], wfi[:, :, fg * 256:(fg + 1) * 256], writes=[bw_])
                k.dma(w_[:, :, 256:512], wfi[:, :, DFF + fg * 256:DFF + (fg + 1) * 256], writes=[bw_])
                issued.append((w_, bw_))
            return issued[idx]

        cxs = [None] * n_t
        cxs[0] = stage1(0)
        stage1b(cxs[0])
        for ti in range(n_t):
            if ti + 1 < n_t:
                cxs[ti + 1] = stage1(ti + 1)
                stage2(cxs[ti], lambda c_=cxs[ti + 1]: stage1b(c_))
            else:
                stage2(cxs[ti], lambda: None)
        P.barrier()
        ph.close()

    def dump_sbuf(name, tile, shape, dt):
        if name in dbg:
            t = nc.dram_tensor(name, list(shape), dt, kind="ExternalOutput").ap()
            k.dma(t, tile[:])

    stop_after = [x for x in dbg if x.startswith("stop")]
    stop_after = stop_after[0][4:] if stop_after else None

    def finish():
        P.barrier()
        P.emit()
        es.close()
        return nc

    for b in range(nseq):
        phase_A(b)
        if stop_after != "A":
            phase_C(b)
            if stop_after != "C":
                phase_S(b, 1)
                phase_S(b, 0)
        if stop_after in ("C", "S"):
            return finish()
        phase_Q(b)
        if stop_after == "Q":
            return finish()
        phase_T1(b)
        if stop_after == "T1":
            return finish()
        phase_T2(b)
        if stop_after == "A":
            dump_sbuf("dtv", dtv, [128, NCH, 64], F32)
            dump_sbuf("modT", modT, [128, 48, NB], F32)
            return finish()
    return finish()


_NSEQ = 2
_S = 4096


def kernel(**inputs):
    B = inputs["x"].shape[0]
    S = inputs["x"].shape[1]
    ncores = 8
    nseq = B // ncores
    nc = build_program(nseq, S)
    shared = prep_shared(inputs, S)
    in_maps = []
    for c in range(ncores):
        m = dict(shared)
        m.update(prep_core(inputs, c, nseq))
        in_maps.append(m)
    res = run_bass_kernel_spmd(nc, in_maps, core_ids=list(range(ncores)))
    out = np.concatenate([np.asarray(r["out"]) for r in res.results], axis=0)
    return out.astype(np.float32)
```

```python
import contextlib
import numpy as np
import concourse.bass as bass
import concourse.mybir as mybir
from concourse.bass_utils import run_bass_kernel_spmd

F32 = mybir.dt.float32
BF16 = mybir.dt.bfloat16
AF = mybir.ActivationFunctionType
ALU = mybir.AluOpType

D = 1024
KC = 8
CTX = 256
GRID_W = 64
EPS = 1e-6
H = 8
QL = 256
KVL = 128
DN = 128
DR = 64
DV = 128
ATTN_SCALE = (DN + DR) ** -0.5
DI = 2048
SH = 32
SG = 4
HPG = 8
DS = 128
CW = 5
CDIM = 3072
DFF = 2816
NFF = 22
IN_SIZES = (QL, KVL, DR, DI, CDIM, 2 * SH, 2 * D)
OFF_Q, OFF_KV, OFF_KR, OFF_Z, OFF_X, OFF_DT, OFF_G = [int(v) for v in np.cumsum((0,) + IN_SIZES)[:-1]]
DIN = sum(IN_SIZES)
OFF_KRS = DIN


class Buf:
    __slots__ = ("name", "w", "rs")

    def __init__(self, name):
        self.name = name
        self.w = None
        self.rs = []


class Op:
    __slots__ = ("eng", "fn", "deps", "sig", "sigval", "dma", "sem", "semval")

    def __init__(self, eng, fn, dma):
        self.eng = eng
        self.fn = fn
        self.deps = []
        self.sig = False
        self.sigval = 0
        self.dma = dma
        self.sem = None
        self.semval = 0


class Prog:
    ENGS = ("pe", "act", "dve", "pool", "sp")
    NDMA = 24

    def __init__(self, nc):
        self.nc = nc
        self.ops = {e: [] for e in self.ENGS}
        self.dma_count = {e: 0 for e in self.ENGS}
        self.dma_last = {}
        self.out_dmas = []

    def add(self, eng, fn, reads=(), writes=(), dma=False):
        op = Op(eng, fn, dma)
        deps = []
        for b in reads:
            if b.w is not None:
                deps.append((b.w, True))
        for b in writes:
            if b.w is not None:
                deps.append((b.w, False))
            for r in b.rs:
                deps.append((r, False))
        for b in writes:
            b.w = op
            b.rs = []
        for b in reads:
            b.rs.append(op)
        if dma:
            k = self.dma_count[eng]
            self.dma_count[eng] = k + 1
            slot = (eng, k % self.NDMA)
            prev = self.dma_last.get(slot)
            if prev is not None:
                deps.append((prev, True))
            self.dma_last[slot] = op
            op.sem = slot
            op.semval = 16 * (k // self.NDMA + 1)
        seen = set()
        for d, raw in deps:
            if d is op or id(d) in seen:
                continue
            if (not d.dma) and (not dma) and d.eng == eng:
                if eng == "pe" or not raw:
                    continue
            seen.add(id(d))
            op.deps.append(d)
        self.ops[eng].append(op)
        return op

    def barrier(self):
        lasts = []
        for e in self.ENGS:
            comp = [o for o in self.ops[e] if not o.dma]
            if comp:
                lasts.append(comp[-1])
        lasts.extend(self.dma_last.values())
        for e in self.ENGS:
            op = Op(e, lambda eng: eng.nop(), False)
            for d in lasts:
                if d.dma or d.eng != e:
                    op.deps.append(d)
            self.ops[e].append(op)

    def emit(self):
        nc = self.nc
        for e in self.ENGS:
            for op in self.ops[e]:
                for d in op.deps:
                    if not d.dma:
                        d.sig = True
        for e in self.ENGS:
            c = 0
            for op in self.ops[e]:
                if op.sig and not op.dma:
                    c += 1
                    op.sigval = c
        with contextlib.ExitStack() as es:
            csem = {e: es.enter_context(nc.semaphore("c_" + e)) for e in self.ENGS}
            dsem = {}
            for e in self.ENGS:
                for i in range(min(self.NDMA, self.dma_count[e])):
                    dsem[(e, i)] = es.enter_context(nc.semaphore("d_%s_%d" % (e, i)))
            block = es.enter_context(nc.Block())

            def run(e, eng):
                waited = {}
                for op in self.ops[e]:
                    need = {}
                    for d in op.deps:
                        if d.dma:
                            key = ("d",) + d.sem
                            s = dsem[d.sem]
                            v = d.semval
                        else:
                            key = ("c", d.eng)
                            s = csem[d.eng]
                            v = d.sigval
                        if need.get(key, (None, 0))[1] < v:
                            need[key] = (s, v)
                    for key, (s, v) in need.items():
                        if waited.get(key, 0) >= v:
                            continue
                        waited[key] = v
                        eng.wait_ge(s, v)
                    ins = op.fn(eng)
                    if op.dma:
                        ins.then_inc(dsem[op.sem], 16)
                    elif op.sig:
                        ins.then_inc(csem[e], 1)

            block.tensor(lambda eng: run("pe", eng))
            block.scalar(lambda eng: run("act", eng))
            block.vector(lambda eng: run("dve", eng))
            block.gpsimd(lambda eng: run("pool", eng))
            block.sync(lambda eng: run("sp", eng))


class Ring:
    def __init__(self, items):
        self.items = items
        self.i = 0

    def next(self):
        it = self.items[self.i % len(self.items)]
        self.i += 1
        return it


def _kc_layout(w):
    K, N = w.shape
    return np.ascontiguousarray(w.reshape(K // 128, 128, N).transpose(1, 0, 2))


def _fm_layout(v):
    return np.ascontiguousarray(v.reshape(-1, 128).T)


def _rope_tables(S):
    rows = S // GRID_W
    row_pos = np.broadcast_to(np.arange(rows)[:, None], (rows, GRID_W)).reshape(-1).astype(np.float32)
    col_pos = np.broadcast_to(np.arange(GRID_W)[None, :], (rows, GRID_W)).reshape(-1).astype(np.float32)
    n_freq = DR // 4
    freqs = (np.float32(10000.0) ** (-np.arange(n_freq, dtype=np.float32) / np.float32(n_freq))).astype(np.float32)
    ang = np.concatenate([row_pos[:, None] * freqs, col_pos[:, None] * freqs], axis=-1).astype(np.float32)
    cos = np.cos(ang).astype(np.float32).T
    sin = np.sin(ang).astype(np.float32).T
    cs = np.stack([np.concatenate([cos, cos], 0), np.concatenate([-sin, sin], 0)], axis=1)
    return np.ascontiguousarray(cs.astype(np.float32))


def _consts(S):
    k = np.arange(128)
    c = {}
    c["ident"] = np.eye(128, dtype=np.float32)
    tri = np.zeros((128, 4, 128), np.float32)
    tri[:, 0, :] = (k[:, None] > k[None, :])
    tri[:, 1, :] = (k[:, None] <= k[None, :])
    tri[:, 2, :] = (k[:, None] < k[None, :])
    tri[:, 3, :] = (k[:, None] >= k[None, :])
    c["tri"] = tri
    c["rope"] = _rope_tables(S)
    return c


def prep_shared(inp, S):
    g = {}
    L = 0
    w_in = np.asarray(inp["w_in"][L], np.float32)
    kr = w_in[:, OFF_KR:OFF_KR + DR]
    w_in_x = np.concatenate([w_in, kr[:, 32:64], kr[:, 0:32]], axis=1)
    g["w_in"] = _kc_layout(w_in_x)
    g["w_ada"] = _kc_layout(np.asarray(inp["w_ada"][L], np.float32))
    b_ada = np.asarray(inp["b_ada"][L], np.float32)
    g["b_ada_fm"] = _fm_layout(b_ada)
    g["b_ada_row"] = b_ada.reshape(1, -1)
    g["wnorm_fm"] = np.ascontiguousarray(np.stack([_fm_layout(np.asarray(inp["w_norm_mix"][L], np.float32)),
                                                   _fm_layout(np.asarray(inp["w_norm_ffn"][L], np.float32))], axis=2))
    g["wq_fm"] = _fm_layout(np.asarray(inp["w_q_norm"][L], np.float32))
    g["wkv_fm"] = _fm_layout(np.asarray(inp["w_kv_norm"][L], np.float32))
    w_uq = np.asarray(inp["w_uq"][L], np.float32).reshape(QL, H, DN + DR)
    rope = w_uq[:, :, DN:]
    w_uq_x = np.concatenate([w_uq.reshape(QL, H * (DN + DR)),
                             np.concatenate([rope[:, :, 32:], rope[:, :, :32]], axis=2).reshape(QL, H * DR)], axis=1)
    g["w_uq"] = _kc_layout(w_uq_x)
    w_ukv = np.asarray(inp["w_ukv"][L], np.float32).reshape(KVL, H, DN + DV)
    g["w_ukT"] = np.ascontiguousarray(w_ukv[:, :, :DN].transpose(2, 1, 0))
    g["w_uv"] = np.ascontiguousarray(w_ukv[:, :, DN:])
    conv_w = np.asarray(inp["conv_w"][L], np.float32)
    g["convw_fm"] = np.ascontiguousarray(conv_w.T.reshape(24, 128, CW).transpose(1, 0, 2))
    conv_b = np.asarray(inp["conv_b"][L], np.float32)
    g["convb_fm"] = _fm_layout(conv_b)
    g["convb_row"] = conv_b.reshape(1, -1)
    g["dtb_row"] = np.asarray(inp["dt_bias"][L], np.float32).reshape(1, 64)
    g["alog_row"] = np.asarray(inp["a_log"][L], np.float32).reshape(1, 64)
    g["dskip_row"] = np.asarray(inp["d_skip"][L], np.float32).reshape(1, 32)
    g["wssd_row"] = np.asarray(inp["w_ssd_norm"][L], np.float32).reshape(1, DI)
    g["wfinal_row"] = np.asarray(inp["w_norm_final"], np.float32).reshape(1, D)
    g["w_o_mla"] = _kc_layout(np.asarray(inp["w_o_mla"][L], np.float32))
    g["w_o_ssd"] = _kc_layout(np.asarray(inp["w_o_ssd"][L], np.float32))
    g["w_out"] = _kc_layout(np.asarray(inp["w_out"][L], np.float32))
    g["w_ffn_in"] = _kc_layout(np.asarray(inp["w_ffn_in"][L], np.float32))
    g["w_down"] = _kc_layout(np.asarray(inp["w_ffn_down"][L], np.float32))
    g.update(_consts(S))
    return g


def prep_core(inp, core, nseq):
    b0 = core * nseq
    m = {}
    m["x"] = np.ascontiguousarray(np.asarray(inp["x"][b0:b0 + nseq], np.float32))
    m["ctx"] = np.ascontiguousarray(np.asarray(inp["ctx"][b0:b0 + nseq], np.float32))
    cs = [np.asarray(inp["c"][b0 + i], np.float32) for i in range(nseq)] + [np.asarray(inp["c_ctx"], np.float32)]
    cT = np.stack([_fm_layout(v) for v in cs], axis=2)
    m["cT"] = np.ascontiguousarray(cT)
    return m


class K:
    def __init__(self, nc):
        self.nc = nc
        self.P = Prog(nc)

    def dma(self, out, in_, reads=(), writes=(), q="sp"):
        return self.P.add(q, lambda e: e.dma_start(out=out, in_=in_), reads, writes, dma=True)

    def mm(self, specs, reads, writes):
        def fn(e):
            ins = None
            for (o, l, r, st, sp) in specs:
                ins = e.matmul(o, lhsT=l, rhs=r, start=st, stop=sp)
            return ins
        return self.P.add("pe", fn, reads, writes)

    def tr(self, specs, ident, reads, writes):
        def fn(e):
            ins = None
            for (o, i) in specs:
                ins = e.transpose(o, i, ident)
            return ins
        return self.P.add("pe", fn, reads, writes)

    def act(self, out, in_, func, reads, writes, scale=None, bias=None, accum=None):
        kw = {}
        if scale is not None:
            kw["scale"] = scale
        if bias is not None:
            kw["bias"] = bias
        if accum is not None:
            kw["accum_out"] = accum
        return self.P.add("act", lambda e: e.activation(out=out, in_=in_, func=func, **kw), reads, writes)

    def tt(self, eng, out, in0, in1, op, reads, writes):
        return self.P.add(eng, lambda e: e.tensor_tensor(out=out, in0=in0, in1=in1, op=op), reads, writes)

    def ts(self, out, in0, s1, s2, op0, op1, reads, writes):
        if s2 is None:
            return self.P.add("dve", lambda e: e.tensor_scalar(out=out, in0=in0, scalar1=s1, scalar2=None, op0=op0), reads, writes)
        return self.P.add("dve", lambda e: e.tensor_scalar(out=out, in0=in0, scalar1=s1, scalar2=s2, op0=op0, op1=op1), reads, writes)

    def stt(self, eng, out, in0, scalar, in1, op0, op1, reads, writes):
        return self.P.add(eng, lambda e: e.scalar_tensor_tensor(out=out, in0=in0, scalar=scalar, in1=in1, op0=op0, op1=op1), reads, writes)

    def copy(self, eng, out, in_, reads, writes):
        if eng == "act":
            return self.act(out, in_, AF.Copy, reads, writes)
        return self.P.add(eng, lambda e: e.tensor_copy(out=out, in_=in_), reads, writes)

    def recip(self, out, in_, reads, writes):
        return self.P.add("dve", lambda e: e.reciprocal(out=out, in_=in_), reads, writes)

    def memset(self, eng, ap, val, writes):
        return self.P.add(eng, lambda e: e.memset(ap, val), (), writes)


def build_program(nseq, S, dbg=()):
    assert S % 512 == 0
    NL = S // 128
    NCH = 2 + NL
    NT = CTX + S
    nc = bass.Bass("TRN2", target_bir_lowering=False)
    k = K(nc)
    P = k.P
    NB = nseq + 1

    def din(name, shape, dt=F32):
        return nc.dram_tensor(name, list(shape), dt, kind="ExternalInput").ap()

    def dscr(name, shape, dt):
        kind = "ExternalOutput" if name in dbg else "Internal"
        return nc.dram_tensor(name, list(shape), dt, kind=kind).ap()

    x_d = din("x", [nseq, S, D])
    ctx_d = din("ctx", [nseq, CTX, D])
    cT_d = din("cT", [128, KC, NB])
    w_in_d = din("w_in", [128, KC, DIN + DR])
    w_ada_d = din("w_ada", [128, KC, 6 * D])
    b_ada_fm_d = din("b_ada_fm", [128, 48])
    b_ada_row_d = din("b_ada_row", [1, 6 * D])
    wnorm_fm_d = din("wnorm_fm", [128, KC, 2])
    wq_fm_d = din("wq_fm", [128, 2])
    wkv_fm_d = din("wkv_fm", [128, 1])
    w_uq_d = din("w_uq", [128, 2, H * (DN + DR) + H * DR])
    w_ukT_d = din("w_ukT", [128, H, KVL])
    w_uv_d = din("w_uv", [128, H, DV])
    convw_fm_d = din("convw_fm", [128, 24, CW])
    convb_fm_d = din("convb_fm", [128, 24])
    convb_row_d = din("convb_row", [1, CDIM])
    dtb_row_d = din("dtb_row", [1, 64])
    alog_row_d = din("alog_row", [1, 64])
    dskip_row_d = din("dskip_row", [1, 32])
    wssd_row_d = din("wssd_row", [1, DI])
    wfinal_row_d = din("wfinal_row", [1, D])
    w_o_mla_d = din("w_o_mla", [128, KC, D])
    w_o_ssd_d = din("w_o_ssd", [128, 16, D])
    w_out_d = din("w_out", [128, KC, D])
    w_ffn_in_d = din("w_ffn_in", [128, KC, 2 * DFF])
    w_down_d = din("w_down", [128, NFF, D])
    ident_d = din("ident", [128, 128])
    tri_d = din("tri", [128, 4, 128])
    rope_d = din("rope", [64, 2, S])
    out_d = nc.dram_tensor("out", [nseq, S, D], F32, kind="ExternalOutput").ap()

    wb = {}
    for name, src in (("w_in", w_in_d), ("w_uq", w_uq_d), ("w_ukT", w_ukT_d), ("w_uv", w_uv_d),
                      ("w_o_mla", w_o_mla_d), ("w_o_ssd", w_o_ssd_d), ("w_out", w_out_d),
                      ("w_ffn_in", w_ffn_in_d), ("w_down", w_down_d)):
        wb[name] = (dscr(name + "_bf", src.shape, BF16), Buf(name + "_bf"), src)
    uTc = [dscr("uTc%d" % b, [CDIM, CTX + 4], BF16) for b in range(nseq)]
    uTl = [dscr("uTl%d" % b, [CDIM, S + 4], BF16) for b in range(nseq)]
    sz_s = [dscr("sz%d" % b, [S, DI], BF16) for b in range(nseq)]
    gT_s = [dscr("gT%d" % b, [2 * D, S], BF16) for b in range(nseq)]
    Qn_s = [dscr("Qn%d" % b, [H, 128, S], BF16) for b in range(nseq)]
    Qr_s = [dscr("Qr%d" % b, [H, DR, S], BF16) for b in range(nseq)]
    xs_s = [dscr("xs%d" % b, [NT, DI], BF16) for b in range(nseq)]
    Bt_s = [dscr("Bt%d" % b, [NT, 512], BF16) for b in range(nseq)]
    BCf_s = [dscr("BCf%d" % b, [1024, NT], BF16) for b in range(nseq)]
    yb_s = [dscr("yb%d" % b, [S, DI], BF16) for b in range(nseq)]
    ygT_s = [dscr("ygT%d" % b, [DI, S], BF16) for b in range(nseq)]
    ymT_s = [dscr("ymT%d" % b, [D, S], BF16) for b in range(nseq)]
    KTn_s = [dscr("KTn%d" % b, [128, NT], BF16) for b in range(nseq)]
    KTr_s = [dscr("KTr%d" % b, [64, NT], BF16) for b in range(nseq)]
    Vt_s = [dscr("Vt%d" % b, [128, NCH, 128], BF16) for b in range(nseq)]
    gb_s = dscr("gb", [2, nseq, D], F32)
    x1_s = [dscr("x1_%d" % b, [S, D], F32) for b in range(nseq)]

    es = contextlib.ExitStack()

    uid = [0]

    def sb(name, shape, dt, stack=None):
        uid[0] += 1
        t = (stack or es).enter_context(nc.sbuf_tensor("s%d_%s" % (uid[0], name), list(shape), dt))
        return t, Buf(name)

    def ps(name, shape, dt=F32, stack=None):
        uid[0] += 1
        t = (stack or es).enter_context(nc.psum_tensor("p%d_%s" % (uid[0], name), list(shape), dt))
        return t, Buf(name)

    ident_f, b_ident_f = sb("ident_f", [128, 128], F32)
    ident_b, b_ident_b = sb("ident_b", [128, 128], BF16)
    tri_f, b_tri = sb("tri_f", [128, 4, 128], F32)
    tri_b, b_tri_b = sb("tri_b", [128, 4, 128], BF16)
    ones_f, b_ones_f = sb("ones_f", [128, 128], F32)
    ones_b, b_ones_b = sb("ones_b", [128, 128], BF16)
    modT, b_modT = sb("modT", [128, 48, NB], F32)
    sc1e, b_sc1e = sb("sc1e", [128, KC, NB], F32)
    sc2e, b_sc2e = sb("sc2e", [128, KC, NB], F32)
    wnorm_fm, b_wnorm = sb("wnorm_fm", [128, KC, 2], F32)
    wq_fm, b_wq = sb("wq_fm", [128, 2], F32)
    wkv_fm, b_wkv = sb("wkv_fm", [128, 1], F32)
    dsk_bc, b_dsk = sb("dsk_bc", [128, 32], F32)
    dtb_bc, b_dtb = sb("dtb_bc", [128, 64], F32)
    a_bc, b_abc = sb("a_bc", [128, 64], F32)
    convb_fm, b_convb_fm = sb("convb_fm", [128, 24], F32)
    convb_rb, b_convb_rb = sb("convb_rb", [1, CDIM], BF16)
    dtv, b_dtv = sb("dtv", [128, NCH, 64], F32)
    zpad, b_zpad = sb("zpad", [128, 24, 2], BF16)

    k.dma(ident_f[:], ident_d, writes=[b_ident_f])
    k.dma(tri_f[:], tri_d, writes=[b_tri])
    k.copy("dve", ident_b[:], ident_f[:], [b_ident_f], [b_ident_b])
    k.copy("dve", tri_b[:], tri_f[:], [b_tri], [b_tri_b])
    k.memset("dve", ones_f[:], 1.0, [b_ones_f])
    k.memset("dve", ones_b[:], 1.0, [b_ones_b])
    k.memset("dve", zpad[:], 0.0, [b_zpad])
    k.dma(wnorm_fm[:], wnorm_fm_d, writes=[b_wnorm])
    k.dma(wq_fm[:], wq_fm_d, writes=[b_wq])
    k.dma(wkv_fm[:], wkv_fm_d, writes=[b_wkv])
    k.dma(convb_fm[:], convb_fm_d, writes=[b_convb_fm])
    k.dma(convb_rb[:], convb_row_d, writes=[b_convb_rb], q="pool")
    k.dma(dsk_bc[:], dskip_row_d.partition_broadcast(128), writes=[b_dsk])
    k.dma(dtb_bc[:], dtb_row_d.partition_broadcast(128), writes=[b_dtb])
    k.dma(a_bc[:], alog_row_d.partition_broadcast(128), writes=[b_abc])
    k.act(a_bc[:], a_bc[:], AF.Exp, [b_abc], [b_abc])
    k.ts(a_bc[:], a_bc[:], -1.0, None, ALU.mult, None, [b_abc], [b_abc])
    def cast_weights(names):
        for name in names:
            dst, bw, src = wb[name]
            nk = src.shape[1]
            step = max(1, nk // 4)
            for k0 in range(0, nk, step):
                k1 = min(nk, k0 + step)
                k.dma(dst[:, k0:k1, :], src[:, k0:k1, :], writes=[Buf("tmp")], q="pool")

    for b in range(nseq):
        for t, n in ((uTc[b], CTX), (uTl[b], S)):
            v = t.rearrange("(c p) t -> p c t", p=128)
            k.dma(v[:, :, 0:2], zpad[:], reads=[b_zpad])
            k.dma(v[:, :, n + 2:n + 4], zpad[:], reads=[b_zpad])
    P.barrier()

    ph = contextlib.ExitStack()
    wada_t, b_wada_t = sb("wada_t", [128, KC, 1536], BF16, ph)
    wada_f, b_wada_f = sb("wada_f", [128, KC, 1536], F32, ph)
    cT_f, b_cT_f = sb("cT_f", [128, KC, NB], F32, ph)
    cT_s, b_cT_s = sb("cT_s", [128, KC, NB], BF16, ph)
    cbc, b_cbc = sb("cbc", [128, KC, nseq, 128], BF16, ph)
    g1bc, b_g1bc = sb("g1bc", [128, nseq, D], F32, ph)
    g2bc, b_g2bc = sb("g2bc", [128, nseq, D], F32, ph)
    bada_fm, b_bada_fm = sb("bada_fm", [128, 48], F32, ph)
    bada_bc, b_bada_bc = sb("bada_bc", [128, 2, D], F32, ph)
    pm, b_pm = ps("pm", [128, 48, NB], F32, ph)
    pg = [ps("pg%d" % i, [128, 512], F32, ph) for i in range(2)]
    k.dma(cT_f[:], cT_d, writes=[b_cT_f])
    k.dma(bada_fm[:], b_ada_fm_d, writes=[b_bada_fm])
    k.dma(bada_bc[:, 0, :], b_ada_row_d[:, 2 * D:3 * D].partition_broadcast(128), writes=[b_bada_bc])
    k.dma(bada_bc[:, 1, :], b_ada_row_d[:, 5 * D:6 * D].partition_broadcast(128), writes=[b_bada_bc])
    k.act(cT_s[:], cT_f[:], AF.Silu, [b_cT_f], [b_cT_s])
    for b in range(nseq):
        k.copy("dve", cbc[:, :, b, :], cT_s[:, :, b:b + 1].to_broadcast([128, KC, 128]), [b_cT_s], [b_cbc])
    for q4 in range(4):
        k.dma(wada_f[:], w_ada_d[:, :, q4 * 1536:(q4 + 1) * 1536], writes=[b_wada_f])
        k.copy("dve", wada_t[:, 0:4, :], wada_f[:, 0:4, :], [b_wada_f], [b_wada_t])
        k.copy("act", wada_t[:, 4:8, :], wada_f[:, 4:8, :], [b_wada_f], [b_wada_t])
        specs = []
        for cc in range(12):
            ch = q4 * 12 + cc
            for kc in range(KC):
                specs.append((pm[:, ch, :], wada_t[:, kc, cc * 128:(cc + 1) * 128], cT_s[:, kc, :], kc == 0, kc == KC - 1))
        k.mm(specs, [b_wada_t, b_cT_s], [b_pm])
        if q4 in (1, 3):
            gdst, gb = (g1bc, b_g1bc) if q4 == 1 else (g2bc, b_g2bc)
            for b in range(nseq):
                for hf in range(2):
                    pt, bp = pg[hf]
                    specs = [(pt[:], cbc[:, kc, b, :], wada_t[:, kc, 512 + hf * 512:1024 + hf * 512], kc == 0, kc == KC - 1)
                             for kc in range(KC)]
                    k.mm(specs, [b_wada_t, b_cbc], [bp])
                    k.tt("dve", gdst[:, b, hf * 512:(hf + 1) * 512], pt[:], bada_bc[:, 0 if q4 == 1 else 1, hf * 512:(hf + 1) * 512],
                         ALU.add, [bp, b_bada_bc], [gb])
    k.tt("dve", modT[:], pm[:], bada_fm[:].unsqueeze(2).to_broadcast([128, 48, NB]), ALU.add, [b_pm, b_bada_fm], [b_modT])
    k.stt("dve", sc1e[:], modT[:, 8:16, :], 1.0, wnorm_fm[:, :, 0:1].to_broadcast([128, KC, NB]), ALU.add, ALU.mult,
          [b_modT, b_wnorm], [b_sc1e])
    k.stt("dve", sc2e[:], modT[:, 32:40, :], 1.0, wnorm_fm[:, :, 1:2].to_broadcast([128, KC, NB]), ALU.add, ALU.mult,
          [b_modT, b_wnorm], [b_sc2e])
    k.dma(gb_s[0:1, :, :], g1bc[0:1, :, :], reads=[b_g1bc])
    k.dma(gb_s[1:2, :, :], g2bc[0:1, :, :], reads=[b_g2bc])
    P.barrier()
    ph.close()

    wdep = {}
    for name in ("w_uq", "w_ukT"):
        dst, _bw, src = wb[name]
        wdep[name] = Buf(name + "_cast")
        k.dma(dst, src, writes=[wdep[name]], q="pool")
    in_groups = [(0, 448), (OFF_KRS, 64), (OFF_DT, 64)] + [(OFF_X + i * 512, 512) for i in range(6)] \
        + [(OFF_Z + i * 512, 512) for i in range(4)] + [(OFF_G + i * 512, 512) for i in range(4)]
    for (c0_, n_) in in_groups:
        dst, _bw, src = wb["w_in"]
        wdep[c0_] = Buf("w_in_cast_%d" % c0_)
        k.dma(dst[:, :, c0_:c0_ + n_], src[:, :, c0_:c0_ + n_], writes=[wdep[c0_]], q="pool")
    cast_weights(("w_uv", "w_o_mla", "w_o_ssd", "w_out", "w_ffn_in", "w_down"))
    eps_t, b_eps = sb("eps_t", [128, 1], F32)
    k.memset("dve", eps_t[:], EPS, [b_eps])
    nhalf, b_nhalf = sb("nhalf", [128, 512], F32)
    k.memset("dve", nhalf[:], -0.5, [b_nhalf])

    def rstd_op(out, in_, n, nfree, reads, writes):
        k.ts(out, in_, 1.0 / n, EPS, ALU.mult, ALU.add, reads, writes)
        k.tt("pool", out, out, nhalf[:, 0:nfree], ALU.pow, list(writes) + [b_nhalf], writes)

    def ring(name, n, shape, dt, stack, psum=False):
        return Ring([(ps if psum else sb)("%s%d" % (name, i), shape, dt, stack) for i in range(n)])

    def tiles_of():
        return [("ctx", 0, 2)] + [("lat", i * 4, 4) for i in range(S // 512)]

    w_in_bf = wb["w_in"][0]

    def phase_A(b):
        ph = contextlib.ExitStack()
        wuq, b_wuq = sb("A_wuq", [128, 2, H * 192 + H * 64], BF16, ph)
        wukT, b_wukT = sb("A_wukT", [128, H, 128], BF16, ph)
        wdt, b_wdt = sb("A_wdt", [128, KC, 64], BF16, ph)
        xt_r = ring("A_xt", 3, [128, D], F32, ph)
        xn_r = ring("A_xn", 2, [128, 4, D], BF16, ph)
        junk, b_junk = sb("A_junk", [128, D], BF16, ph)
        st_r = ring("A_st", 2, [128, 12], F32, ph)
        hT_r = ring("A_hT", 2, [128, KC, 512], BF16, ph)
        wg_r = ring("A_wg", 4, [128, KC, 512], BF16, ph)
        cs_r = ring("A_cs", 2, [64, 2, 512], F32, ph)
        cq_sb, b_cq = sb("A_cq", [128, 2, 512], F32, ph)
        sq, b_sq = sb("A_sq", [128, 2, 512], BF16, ph)
        rq, b_rq = sb("A_rq", [128, 512], F32, ph)
        cqn, b_cqn = sb("A_cqn", [128, 2, 512], BF16, ph)
        ckv_sb, b_ckv = sb("A_ckv", [128, 512], F32, ph)
        sqk, b_sqk = sb("A_sqk", [128, 512], BF16, ph)
        rk, b_rk = sb("A_rk", [128, 512], F32, ph)
        qn_r = ring("A_qn", 2, [128, 512], BF16, ph)
        Qn_st, b_Qn_st = sb("A_Qn_st", [128, H, 512], BF16, ph)
        Qr_st, b_Qr_st = sb("A_Qr_st", [64, H, 512], BF16, ph)
        t1_r = ring("A_t1", 2, [64, 512], F32, ph)
        t2_r = ring("A_t2", 2, [64, 512], F32, ph)
        zst_r = ring("A_zst", 3, [128, 512], BF16, ph)
        ust_r = ring("A_ust", 2, [128, 4, 512], BF16, ph)
        gst_r = ring("A_gst", 2, [128, 4, 512], BF16, ph)
        dtt_r = ring("A_dtt", 2, [128, 64], F32, ph)
        acc_r = ring("A_acc", 5, [128, 512], F32, ph, psum=True)
        KTn, b_KTn = sb("KTn", [128, NT], BF16, ph)
        KTr, b_KTr = sb("KTr", [64, NT], BF16, ph)
        Vt, b_Vt = sb("Vt", [128, NCH, 128], BF16, ph)
        pT_r = ring("A_pT", 2, [128, 512], BF16, ph, psum=True)

        k.dma(wuq[:], wb["w_uq"][0], reads=[wdep["w_uq"]], writes=[b_wuq])
        k.dma(wukT[:], wb["w_ukT"][0], reads=[wdep["w_ukT"]], writes=[b_wukT])
        k.dma(wdt[:], w_in_bf[:, :, OFF_DT:OFF_DT + 64], reads=[wdep[OFF_DT]], writes=[b_wdt])
        flip = [0]

        def evac_copy(out, in_, reads, writes):
            flip[0] ^= 1
            return k.copy("act" if flip[0] else "dve", out, in_, reads, writes)

        def fm_specs(pt, M, wt, col0, hT, T):
            return [(pt[0:M, 0:T], wt[:, kc, col0:col0 + M], hT[:, kc, 0:T], kc == 0, kc == KC - 1) for kc in range(KC)]

        TT = [0]

        def rms_bcast(sq_tiles, nfeat, out_r, b_out, reads):
            pr, bpr = acc_r.next()
            T_ = TT[0]
            n = len(sq_tiles)
            k.mm([(pr[:, 0:T_], ones_b[:], s, i == 0, i == n - 1) for i, s in enumerate(sq_tiles)], reads + [b_ones_b], [bpr])
            k.act(out_r[:, 0:T_], pr[:, 0:T_], AF.Sqrt, [bpr, b_eps], [b_out], scale=1.0 / nfeat, bias=eps_t[:, 0:1])
            k.recip(out_r[:, 0:T_], out_r[:, 0:T_], [b_out], [b_out])

        def stage1(kind, i0, nch):
            lat = kind == "lat"
            T_ = nch * 128
            mi = b if lat else nseq
            src = x_d[b] if lat else ctx_d[b]
            t0 = i0 * 128 if lat else 0
            c0 = 2 + i0 if lat else 0
            g0 = c0 * 128
            uT = uTl[b] if lat else uTc[b]
            hT, b_hT = hT_r.next()
            xn, b_xn = xn_r.next()
            st, b_st = st_r.next()
            k.memset("dve", st[:], 0.0, [b_st])
            if lat:
                cs, b_cs = cs_r.next()
                k.dma(cs[:, :, 0:T_], rope_d[:, :, t0:t0 + T_], writes=[b_cs])
            for j in range(nch):
                xt, b_xt = xt_r.next()
                k.dma(xt[:], src[t0 + j * 128:t0 + (j + 1) * 128, :], writes=[b_xt])
                k.act(junk[:], xt[:], AF.Square, [b_xt], [b_junk, b_st], accum=st[:, j:j + 1])
                rstd_op(st[:, 8 + j:9 + j], st[:, j:j + 1], D, 1, [b_st], [b_st])
                k.tt("pool", xn[:, j, :], xt[:], st[:, 8 + j:9 + j].to_broadcast([128, D]), ALU.mult, [b_xt, b_st], [b_xn])
            return dict(lat=lat, T_=T_, mi=mi, t0=t0, c0=c0, g0=g0, uT=uT, hT=hT, b_hT=b_hT, cs=(cs if lat else None), b_cs=(b_cs if lat else None), nch=nch,
                        xn=xn, b_xn=b_xn)

        def stage1b(cx):
            T_ = cx['T_']; mi = cx['mi']; nch = cx['nch']; hT = cx['hT']; b_hT = cx['b_hT']; xn = cx['xn']; b_xn = cx['b_xn']
            for kc in range(KC):
                pT, b_pT = pT_r.next()
                k.tr([(pT[:, j * 128:(j + 1) * 128], xn[:, j, kc * 128:(kc + 1) * 128]) for j in range(nch)], ident_b[:],
                     [b_xn, b_ident_b], [b_pT])
                if kc % 2 == 0:
                    k.act(hT[:, kc, 0:T_], pT[:, 0:T_], AF.Identity, [b_pT, b_sc1e, b_modT], [b_hT],
                          scale=sc1e[:, kc, mi:mi + 1], bias=modT[:, kc, mi:mi + 1])
                else:
                    k.ts(hT[:, kc, 0:T_], pT[:, 0:T_], sc1e[:, kc, mi:mi + 1], modT[:, kc, mi:mi + 1], ALU.mult, ALU.add,
                         [b_pT, b_sc1e, b_modT], [b_hT])

        def stage2(cx, mid_hook):
            lat = cx['lat']; T_ = cx['T_']; mi = cx['mi']; t0 = cx['t0']; c0 = cx['c0']; g0 = cx['g0']; uT = cx['uT']
            hT = cx['hT']; b_hT = cx['b_hT']; cs = cx['cs']; b_cs = cx['b_cs']; nch = cx['nch']
            TT[0] = T_
            wg, b_wg = get_wg()
            pt, bpt = acc_r.next()
            k.mm(fm_specs(pt, 128, wg, OFF_KV, hT, T_), [b_wg, b_hT], [bpt])
            k.act(ckv_sb[:, 0:T_], pt[:, 0:T_], AF.Copy, [bpt], [b_ckv])
            k.act(sqk[:, 0:T_], pt[:, 0:T_], AF.Square, [bpt], [b_sqk])
            rms_bcast([sqk[:, 0:T_]], KVL, rk, b_rk, [b_sqk])
            k.stt("dve", KTn[:, g0:g0 + T_], ckv_sb[:, 0:T_], wkv_fm[:, 0:1], rk[:, 0:T_], ALU.mult, ALU.mult,
                  [b_ckv, b_wkv, b_rk], [b_KTn])
            pT, b_pT = pT_r.next()
            k.tr([(pT[:, j * 128:(j + 1) * 128], KTn[:, g0 + j * 128:g0 + (j + 1) * 128]) for j in range(nch)], ident_b[:],
                 [b_KTn, b_ident_b], [b_pT])
            k.copy("dve", Vt[:, c0:c0 + nch, :], pT[:, 0:T_].rearrange("p (j c) -> p j c", j=nch), [b_pT], [b_Vt])
            p1, bp1 = acc_r.next()
            k.mm(fm_specs(p1, 64, wg, OFF_KR, hT, T_), [b_wg, b_hT], [bp1])
            if lat:
                p2, bp2 = acc_r.next()
                k.mm(fm_specs(p2, 64, wg, 448, hT, T_), [b_wg, b_hT], [bp2])
                t1, b_t1 = t1_r.next()
                t2, b_t2 = t2_r.next()
                k.tt("dve", t1[:, 0:T_], p1[0:64, 0:T_], cs[:, 0, 0:T_], ALU.mult, [bp1, b_cs], [b_t1])
                k.tt("dve", t2[:, 0:T_], p2[0:64, 0:T_], cs[:, 1, 0:T_], ALU.mult, [bp2, b_cs], [b_t2])
                k.tt("pool", KTr[:, g0:g0 + T_], t1[:, 0:T_], t2[:, 0:T_], ALU.add, [b_t1, b_t2], [b_KTr])
            else:
                k.copy("act", KTr[:, g0:g0 + T_], p1[0:64, 0:T_], [bp1], [b_KTr])
            if lat:
                for jq in range(2):
                    pt, bpt = acc_r.next()
                    k.mm(fm_specs(pt, 128, wg, jq * 128, hT, T_), [b_wg, b_hT], [bpt])
                    k.act(cq_sb[:, jq, 0:T_], pt[:, 0:T_], AF.Copy, [bpt], [b_cq])
                    k.act(sq[:, jq, 0:T_], pt[:, 0:T_], AF.Square, [bpt], [b_sq])
                rms_bcast([sq[:, 0, 0:T_], sq[:, 1, 0:T_]], QL, rq, b_rq, [b_sq])
                for jq in range(2):
                    k.stt("dve", cqn[:, jq, 0:T_], cq_sb[:, jq, 0:T_], wq_fm[:, jq:jq + 1], rq[:, 0:T_], ALU.mult, ALU.mult,
                          [b_cq, b_wq, b_rq], [b_cqn])
                qns = {}

                def q1(h):
                    pt, bpt = acc_r.next()
                    k.mm([(pt[:, 0:T_], wuq[:, kc, h * 192:h * 192 + 128], cqn[:, kc, 0:T_], kc == 0, kc == 1) for kc in range(2)],
                         [b_wuq, b_cqn], [bpt])
                    qn, b_qn = qn_r.next()
                    evac_copy(qn[:, 0:T_], pt[:, 0:T_], [bpt], [b_qn])
                    qns[h] = (qn, b_qn)

                def q2(h):
                    qn, b_qn = qns.pop(h)
                    pa, bpa = acc_r.next()
                    k.mm([(pa[:, 0:T_], wukT[:, h, :], qn[:, 0:T_], True, True)], [b_wukT, b_qn], [bpa])
                    evac_copy(Qn_st[:, h, 0:T_], pa[:, 0:T_], [bpa], [b_Qn_st])

                def q3(h):
                    p1, bp1 = acc_r.next()
                    k.mm([(p1[0:64, 0:T_], wuq[:, kc, h * 192 + 128:h * 192 + 192], cqn[:, kc, 0:T_], kc == 0, kc == 1) for kc in range(2)],
                         [b_wuq, b_cqn], [bp1])
                    p2, bp2 = acc_r.next()
                    k.mm([(p2[0:64, 0:T_], wuq[:, kc, H * 192 + h * 64:H * 192 + (h + 1) * 64], cqn[:, kc, 0:T_], kc == 0, kc == 1) for kc in range(2)],
                         [b_wuq, b_cqn], [bp2])
                    t1, b_t1 = t1_r.next()
                    t2, b_t2 = t2_r.next()
                    k.tt("dve", t1[:, 0:T_], p1[0:64, 0:T_], cs[:, 0, 0:T_], ALU.mult, [bp1, b_cs], [b_t1])
                    k.tt("dve", t2[:, 0:T_], p2[0:64, 0:T_], cs[:, 1, 0:T_], ALU.mult, [bp2, b_cs], [b_t2])
                    k.tt("pool", Qr_st[:, h, 0:T_], t1[:, 0:T_], t2[:, 0:T_], ALU.add, [b_t1, b_t2], [b_Qr_st])

                q1(0)
                for h in range(H):
                    if h + 1 < H:
                        q1(h + 1)
                    q3(h)
                    q2(h)
                k.dma(Qn_s[b].rearrange("h p t -> p h t")[:, :, t0:t0 + T_], Qn_st[:, :, 0:T_], reads=[b_Qn_st])
                k.dma(Qr_s[b].rearrange("h p t -> p h t")[:, :, t0:t0 + T_], Qr_st[:, :, 0:T_], reads=[b_Qr_st])
            for j in range(nch):
                pt, bpt = acc_r.next()
                k.mm([(pt[:, 0:64], hT[:, kc, j * 128:(j + 1) * 128], wdt[:, kc, :], kc == 0, kc == KC - 1) for kc in range(KC)],
                     [b_wdt, b_hT], [bpt])
                dtt, b_dtt = dtt_r.next()
                k.tt("dve", dtt[:], pt[:, 0:64], dtb_bc[:], ALU.add, [bpt, b_dtb], [b_dtt])
                k.act(dtt[:], dtt[:], AF.Exp, [b_dtt], [b_dtt])
                k.act(dtv[:, c0 + j, :], dtt[:], AF.Ln, [b_dtt], [b_dtv], bias=1.0)
            for xg in range(6):
                wg, b_wg = get_wg()
                ust, b_ust = ust_r.next()
                for cc in range(4):
                    pt, bpt = acc_r.next()
                    k.mm(fm_specs(pt, 128, wg, cc * 128, hT, T_), [b_wg, b_hT], [bpt])
                    evac_copy(ust[:, cc, 0:T_], pt[:, 0:T_], [bpt], [b_ust])
                k.dma(uT.rearrange("(c p) t -> p c t", p=128)[:, xg * 4:(xg + 1) * 4, 2 + t0:2 + t0 + T_], ust[:, :, 0:T_], reads=[b_ust])
                if xg == 2:
                    mid_hook()
            if lat:
                for zg in range(4):
                    wg, b_wg = get_wg()
                    for j in range(nch):
                        pt, bpt = acc_r.next()
                        k.mm([(pt[:], hT[:, kc, j * 128:(j + 1) * 128], wg[:, kc, :], kc == 0, kc == KC - 1) for kc in range(KC)],
                             [b_wg, b_hT], [bpt])
                        zst, b_zst = zst_r.next()
                        k.act(zst[:], pt[:], AF.Silu, [bpt], [b_zst])
                        k.dma(sz_s[b][t0 + j * 128:t0 + (j + 1) * 128, zg * 512:(zg + 1) * 512], zst[:], reads=[b_zst])
                for gg in range(4):
                    wg, b_wg = get_wg()
                    gst, b_gst = gst_r.next()
                    for cc in range(4):
                        pt, bpt = acc_r.next()
                        k.mm(fm_specs(pt, 128, wg, cc * 128, hT, T_), [b_wg, b_hT], [bpt])
                        k.act(gst[:, cc, 0:T_], pt[:, 0:T_], AF.Sigmoid, [bpt], [b_gst])
                    k.dma(gT_s[b].rearrange("(c p) t -> p c t", p=128)[:, gg * 4:(gg + 1) * 4, t0:t0 + T_], gst[:, :, 0:T_], reads=[b_gst])
        tl = tiles_of()
        gspecs = []
        for (kind_, _i0, _n) in tl:
            gspecs += [[(0, 448, 0), (448, 64, OFF_KRS)]] + [[(0, 512, OFF_X + xg * 512)] for xg in range(6)]
            if kind_ == "lat":
                gspecs += [[(0, 512, OFF_Z + zg * 512)] for zg in range(4)] + [[(0, 512, OFF_G + gg * 512)] for gg in range(4)]
        issued = []
        gi = [0]

        def get_wg():
            idx = gi[0]
            gi[0] += 1
            while len(issued) < min(len(gspecs), idx + 3):
                w_, bw_ = wg_r.next()
                for (d0, n_, s0) in gspecs[len(issued)]:
                    k.dma(w_[:, :, d0:d0 + n_], w_in_bf[:, :, s0:s0 + n_], reads=[wdep[s0]], writes=[bw_])
                issued.append((w_, bw_))
            return issued[idx]

        cxs = [None] * len(tl)
        cxs[0] = stage1(*tl[0])
        stage1b(cxs[0])
        for ti in range(len(tl)):
            if ti + 1 < len(tl):
                cxs[ti + 1] = stage1(*tl[ti + 1])
                stage2(cxs[ti], lambda c_=cxs[ti + 1]: stage1b(c_))
            else:
                stage2(cxs[ti], lambda: None)
        k.dma(KTn_s[b], KTn[:], reads=[b_KTn])
        k.dma(KTr_s[b], KTr[:], reads=[b_KTr])
        k.dma(Vt_s[b], Vt[:], reads=[b_Vt])
        P.barrier()
        ph.close()

    def phase_C(b):
        ph = contextlib.ExitStack()
        convw, b_convw = sb("C_convw", [128, 24, CW], F32, ph)
        diag, b_diag = sb("C_diag", [128, 24, CW, 128], BF16, ph)
        uw_r = ring("C_uw", 2, [128, 24, 516], BF16, ph)
        st_r = ring("C_st", 4, [128, 512], BF16, ph)
        acc_r = ring("C_acc", 6, [128, 512], F32, ph, psum=True)
        k.dma(convw[:], convw_fm_d, writes=[b_convw])
        k.tt("pool", diag[:], ident_f[:].unsqueeze(1).unsqueeze(1).to_broadcast([128, 24, CW, 128]),
             convw[:].unsqueeze(3).to_broadcast([128, 24, CW, 128]), ALU.mult, [b_ident_f, b_convw], [b_diag])
        for (kind, i0, nch) in tiles_of():
            lat = kind == "lat"
            T_ = nch * 128
            t0 = i0 * 128 if lat else 0
            g0 = (2 + i0) * 128 if lat else 0
            uT = uTl[b] if lat else uTc[b]
            uw, b_uw = uw_r.next()
            k.dma(uw[:, :, 0:T_ + 4], uT.rearrange("(c p) t -> p c t", p=128)[:, :, t0:t0 + T_ + 4], writes=[b_uw])
            for j in range(nch):
                for grp in range(5):
                    ch0 = grp * 4
                    pt, bpt = acc_r.next()
                    specs = [(pt[:], ones_b[0:1, 0:128], convb_rb[0:1, ch0 * 128:ch0 * 128 + 512], True, False)]
                    for cc in range(4):
                        for tap in range(CW):
                            specs.append((pt[:, cc * 128:(cc + 1) * 128], uw[:, ch0 + cc, j * 128 + tap:j * 128 + tap + 128],
                                          diag[:, ch0 + cc, tap, :], False, tap == CW - 1))
                    k.mm(specs, [b_ones_b, b_convb_rb, b_uw, b_diag], [bpt])
                    stt_, b_stt = st_r.next()
                    k.act(stt_[:], pt[:], AF.Silu, [bpt], [b_stt])
                    if grp < 4:
                        k.dma(xs_s[b][g0 + j * 128:g0 + (j + 1) * 128, grp * 512:(grp + 1) * 512], stt_[:], reads=[b_stt])
                    else:
                        k.dma(Bt_s[b][g0 + j * 128:g0 + (j + 1) * 128, :], stt_[:], reads=[b_stt])
            for ch in range(16, 24):
                pt, bpt = acc_r.next()
                k.mm([(pt[:, 0:T_], diag[:, ch, tap, :], uw[:, ch, tap:tap + T_], tap == 0, tap == CW - 1) for tap in range(CW)],
                     [b_uw, b_diag], [bpt])
                stt_, b_stt = st_r.next()
                k.act(stt_[:, 0:T_], pt[:, 0:T_], AF.Silu, [bpt, b_convb_fm], [b_stt], bias=convb_fm[:, ch:ch + 1])
                k.dma(BCf_s[b][(ch - 16) * 128:(ch - 15) * 128, g0:g0 + T_], stt_[:, 0:T_], reads=[b_stt])
        P.barrier()
        ph.close()

    def phase_S(b, d):
        fwd = d == 0
        ph = contextlib.ExitStack()
        iu, im = (0, 1) if fwd else (2, 3)
        U = tri_f[:, iu, :]
        Ub = tri_b[:, iu, :]
        TMk = tri_f[:, im, :]
        hs = slice(d * 32, (d + 1) * 32)
        S32, _ = sb("S_S32", [128, DI], F32, ph)
        Sbf, _ = sb("S_Sbf", [128, DI], BF16, ph)
        bS32 = [Buf("S32_%d" % g) for g in range(4)]
        bSbf = [Buf("Sbf_%d" % g) for g in range(4)]
        xs_r = ring("S_xs", 3, [128, DI], BF16, ph)
        Bt_r = ring("S_Bt", 2, [128, 512], BF16, ph)
        BC_r = ring("S_BC", 2, [128, 8, 128], BF16, ph)
        sm_r = ring("S_sm", 3, [128, 8, 32], F32, ph)
        adb_r = ring("S_adb", 3, [128, 32], BF16, ph)
        xd_r = ring("S_xd", 2, [128, DI], BF16, ph)
        xdd_r = ring("S_xdd", 2, [128, DI], BF16, ph)
        Gm_r = ring("S_Gm", 2, [128, 4, 128], BF16, ph)
        R_r = ring("S_R", 2, [128, 2, 8, 128], BF16, ph)
        E_r = ring("S_E", 2, [128, 8, 128], BF16, ph)
        W_r = ring("S_W", 2, [128, 4, 8, 128], BF16, ph)
        tO_r = ring("S_tO", 2, [128, 512], F32, ph)
        ys_r = ring("S_ys", 2, [128, DI], F32 if fwd else BF16, ph)
        psm_r = ring("S_psm", 1, [128, 512], F32, ph, psum=True)
        pG_r = ring("S_pG", 1, [128, 512], F32, ph, psum=True)
        pa_r = ring("S_pa", 2, [128, 512], F32, ph, psum=True)
        pb_r = ring("S_pb", 3, [128, 512], F32, ph, psum=True)
        if fwd:
            DIh, b_DIh = sb("S_DIh", [128, 32, 128], BF16, ph)
            yb_r = ring("S_yb", 2, [128, DI], BF16, ph)
            sz_r = ring("S_sz", 3, [128, DI], BF16, ph)
            yg, b_yg = sb("S_yg", [128, DI], F32, ph)
            ygn, b_ygn = sb("S_ygn", [128, DI], BF16, ph)
            junk, b_junk = sb("S_junk", [128, DI], BF16, ph)
            st4, b_st4 = sb("S_st4", [128, 4], F32, ph)
            wssd_bc, b_wssd = sb("S_wssd", [128, DI], F32, ph)
            gst, b_gst = sb("S_gst", [128, 16, 512], BF16, ph)
            pT_r = ring("S_pT", 1, [128, 1024], BF16, ph, psum=True)
            k.dma(wssd_bc[:], wssd_row_d.partition_broadcast(128), writes=[b_wssd])
            k.tt("dve", DIh[:], ident_f[:].unsqueeze(1).to_broadcast([128, 32, 128]), dsk_bc[:].unsqueeze(2).to_broadcast([128, 32, 128]),
                 ALU.mult, [b_ident_f, b_dsk], [b_DIh])
        k.memset("dve", S32[:], 0.0, bS32)
        k.memset("dve", Sbf[:], 0.0, bSbf)
        order = list(range(NCH)) if fwd else [1, 0] + list(range(NCH - 1, 1, -1))

        def prep_head(ci):
            c = order[ci]
            cx = dict(c=c, lat=c >= 2, last=ci == len(order) - 1, g0=c * 128, t0=(c - 2) * 128, late=[])
            lat, last, g0, t0 = cx["lat"], cx["last"], cx["g0"], cx["t0"]
            xs, b_xs = xs_r.next()
            k.dma(xs[:], xs_s[b][g0:g0 + 128, :], writes=[b_xs])
            cx.update(xs=xs, b_xs=b_xs)
            if not last:
                Bt, b_Bt = Bt_r.next()
                k.dma(Bt[:], Bt_s[b][g0:g0 + 128, :], writes=[b_Bt])
                cx.update(Bt=Bt, b_Bt=b_Bt)
            if lat:
                BC, b_BC = BC_r.next()
                k.dma(BC[:], BCf_s[b].rearrange("(c p) t -> p c t", p=128)[:, :, g0:g0 + 128], writes=[b_BC])
                cx.update(BC=BC, b_BC=b_BC)
                if fwd:
                    ybt, b_ybt = yb_r.next()
                    k.dma(ybt[:], yb_s[b][t0:t0 + 128, :], writes=[b_ybt])
                    szt, b_szt = sz_r.next()
                    k.dma(szt[:], sz_s[b][t0:t0 + 128, :], writes=[b_szt])
                    cx.update(ybt=ybt, b_ybt=b_ybt, szt=szt, b_szt=b_szt)
            sm, b_sm = sm_r.next()
            ad, acs, EA, cd, tmp, wS = (sm[:, i, :] for i in range(6))
            cx.update(sm=sm, b_sm=b_sm)
            dtc = dtv[:, c, hs]
            k.tt("dve", ad, dtc, a_bc[:, hs], ALU.mult, [b_dtv, b_abc], [b_sm])
            adb, b_adb = adb_r.next()
            k.copy("dve", adb[:], ad, [b_sm], [b_adb])
            k.copy("dve", sm[:, 6, :], adb[:], [b_adb], [b_sm])
            k.tt("dve", sm[:, 7, :], ad, sm[:, 6, :], ALU.subtract, [b_sm], [b_sm])
            psm, b_psm = psm_r.next()
            k.mm([(psm[:, 0:32], TMk, ad, True, True), (psm[:, 32:64], ones_f[:], ad, True, True)], [b_tri, b_ones_f, b_sm], [b_psm])
            k.act(acs, psm[:, 0:32], AF.Copy, [b_psm], [b_sm])
            k.act(EA, psm[:, 0:32], AF.Exp, [b_psm], [b_sm])
            k.act(cd, psm[:, 32:64], AF.Exp, [b_psm], [b_sm])
            k.tt("dve", tmp, psm[:, 32:64], acs, ALU.subtract, [b_psm, b_sm], [b_sm])
            k.act(tmp, tmp, AF.Exp, [b_sm], [b_sm])
            k.tt("dve", wS, tmp, dtc, ALU.mult, [b_sm, b_dtv], [b_sm])
            xs3 = xs[:].rearrange("p (h q) -> p h q", h=32)
            if lat:
                pG, b_pG = pG_r.next()
                k.mm([(pG[:, g * 128:(g + 1) * 128], BC[:, g, :], BC[:, 4 + g, :], True, True) for g in range(4)], [b_BC], [b_pG])
                Gm, b_Gm = Gm_r.next()
                k.tt("dve", Gm[:], pG[:].rearrange("p (g l) -> p g l", g=4), TMk.unsqueeze(1).to_broadcast([128, 4, 128]), ALU.mult,
                     [b_pG, b_tri], [b_Gm])
                W4, b_W4 = W_r.next()
                xd, b_xd = xd_r.next()
                cx.update(W4=W4, b_W4=b_W4, xd=xd, b_xd=b_xd, Gm=Gm, b_Gm=b_Gm, Rs={})
                cx["late"].append(lambda: k.tt("pool", xd[:].rearrange("p (h q) -> p h q", h=32), xs3,
                                               dtc.unsqueeze(2).to_broadcast([128, 32, 64]), ALU.mult, [b_xs, b_dtv], [b_xd]))
            if not last:
                xdd, b_xdd = xdd_r.next()
                cx.update(xdd=xdd, b_xdd=b_xdd)
                cx["late"].append(lambda: k.tt("pool", xdd[:].rearrange("p (h q) -> p h q", h=32), xs3,
                                               wS.unsqueeze(2).to_broadcast([128, 32, 64]), ALU.mult, [b_xs, b_sm], [b_xdd]))
            return cx

        def prep_R(cx, g):
            sm, b_sm = cx["sm"], cx["b_sm"]
            ad = sm[:, 0, :]
            Rg, b_Rg = R_r.next()
            for part in range(2):
                adp = sm[:, 6 + part, :]
                if g % 2 == 1:
                    k.tt("pool", Rg[:, part, :, :], adp[:, g * 8:(g + 1) * 8].unsqueeze(2).to_broadcast([128, 8, 128]),
                         TMk.unsqueeze(1).to_broadcast([128, 8, 128]), ALU.mult, [b_sm, b_tri], [b_Rg])
                else:
                    for r in range(8):
                        k.act(Rg[:, part, r, :], TMk, AF.Identity, [b_sm, b_tri], [b_Rg], scale=adp[:, g * 8 + r:g * 8 + r + 1])
            cx["Rs"][g] = (Rg, b_Rg)

        def prep_G(cx, g):
            Rg, b_Rg = cx["Rs"].pop(g)
            W4, b_W4, Gm, b_Gm = cx["W4"], cx["b_W4"], cx["Gm"], cx["b_Gm"]
            Eg, b_Eg = E_r.next()
            for hf in range(2):
                pa, b_pa = pa_r.next()
                k.mm([(pa[:], Ub, Rg[:, 0, hf * 4:(hf + 1) * 4, :], True, False),
                      (pa[:], Ub, Rg[:, 1, hf * 4:(hf + 1) * 4, :], False, True)], [b_tri_b, b_Rg], [b_pa])
                k.act(Eg[:, hf * 4:(hf + 1) * 4, :], pa[:].rearrange("p (r l) -> p r l", r=4), AF.Exp, [b_pa], [b_Eg])
            k.tt("dve", W4[:, g, :, :], Eg[:], Gm[:, g, :].unsqueeze(1).to_broadcast([128, 8, 128]), ALU.mult, [b_Eg, b_Gm], [b_W4])

        def main_G(cx, g):
            c, lat, last, t0 = cx["c"], cx["lat"], cx["last"], cx["t0"]
            sm, b_sm = cx["sm"], cx["b_sm"]
            ad, acs, EA, cd, tmp, wS = (sm[:, i, :] for i in range(6))
            gs = slice(g * 512, (g + 1) * 512)
            hg = slice(g * 8, (g + 1) * 8)
            if lat:
                BC, b_BC, W4, b_W4, xd, b_xd = cx["BC"], cx["b_BC"], cx["W4"], cx["b_W4"], cx["xd"], cx["b_xd"]
                if g == 0:
                    cx["ys"], cx["b_ys"] = ys_r.next()
                ys, b_ys = cx["ys"], cx["b_ys"]
                pY, b_pY = pb_r.next()
                specs = []
                rd_ = [b_ident_b, b_W4, b_xd]
                if fwd:
                    specs.append((pY[:], ident_b[:], cx["ybt"][:, gs], True, False))
                    rd_ += [cx["b_ybt"], b_DIh, cx["b_xs"]]
                for r in range(8):
                    hh = g * 8 + r
                    if fwd:
                        specs.append((pY[:, r * 64:(r + 1) * 64], DIh[:, hh, :], cx["xs"][:, hh * 64:(hh + 1) * 64], False, False))
                    specs.append((pY[:, r * 64:(r + 1) * 64], W4[:, g, r, :], xd[:, hh * 64:(hh + 1) * 64], not fwd, True))
                k.mm(specs, rd_, [b_pY])
                pO, b_pO = pb_r.next()
                k.mm([(pO[:], BC[:, 4 + g, :], Sbf[:, gs], True, True)], [b_BC, bSbf[g]], [b_pO])
                tO, b_tO = tO_r.next()
                k.tt("dve", tO[:].rearrange("p (r q) -> p r q", r=8), pO[:].rearrange("p (r q) -> p r q", r=8),
                     EA[:, hg].unsqueeze(2).to_broadcast([128, 8, 64]), ALU.mult, [b_pO, b_sm], [b_tO])
                k.tt("dve", ys[:, gs], pY[:], tO[:], ALU.add, [b_pY, b_tO], [b_ys])
            if not last:
                Bt, b_Bt, xdd, b_xdd = cx["Bt"], cx["b_Bt"], cx["xdd"], cx["b_xdd"]
                pS, b_pS = pb_r.next()
                k.mm([(pS[:], Bt[:, g * 128:(g + 1) * 128], xdd[:, gs], True, True)], [b_Bt, b_xdd], [b_pS])
                k.tt("pool", S32[:, gs].rearrange("p (r q) -> p r q", r=8), S32[:, gs].rearrange("p (r q) -> p r q", r=8),
                     cd[:, hg].unsqueeze(2).to_broadcast([128, 8, 64]), ALU.mult, [bS32[g], b_sm], [bS32[g]])
                k.tt("dve", S32[:, gs], pS[:], S32[:, gs], ALU.add, [b_pS, bS32[g]], [bS32[g]])
                k.copy("act", Sbf[:, gs], S32[:, gs], [bS32[g]], [bSbf[g]])

        def post_parts(cx):
            c, t0 = cx["c"], cx["t0"]
            ys, b_ys, szt, b_szt = cx["ys"], cx["b_ys"], cx["szt"], cx["b_szt"]
            j = (c - 2) % 4

            def p1():
                k.tt("dve", yg[:], ys[:], szt[:], ALU.mult, [b_ys, b_szt], [b_yg])
                k.memset("dve", st4[:], 0.0, [b_st4])
                k.act(junk[:], yg[:], AF.Square, [b_yg], [b_junk, b_st4], accum=st4[:, 0:1])
                rstd_op(st4[:, 2:3], st4[:, 0:1], DI, 1, [b_st4], [b_st4])

            def p2():
                k.stt("dve", ygn[:], yg[:], st4[:, 2:3], wssd_bc[:], ALU.mult, ALU.mult, [b_yg, b_st4, b_wssd], [b_ygn])

            def p3():
                for hf in range(2):
                    pT, b_pT = pT_r.next()
                    k.tr([(pT[:, i * 128:(i + 1) * 128], ygn[:, (hf * 8 + i) * 128:(hf * 8 + i + 1) * 128]) for i in range(8)], ident_b[:],
                         [b_ygn, b_ident_b], [b_pT])
                    k.copy("act", gst[:, hf * 8:(hf + 1) * 8, j * 128:(j + 1) * 128], pT[:].rearrange("p (i t) -> p i t", i=8), [b_pT], [b_gst])
                if j == 3:
                    tt0 = t0 - 384
                    k.dma(ygT_s[b].rearrange("(c p) t -> p c t", p=128)[:, :, tt0:tt0 + 512], gst[:], reads=[b_gst])
            return [p1, p2, p3]

        n_o = len(order)
        cur = prep_head(0)
        if cur["lat"]:
            prep_R(cur, 0)
            for g in range(4):
                if g < 3:
                    prep_R(cur, g + 1)
                prep_G(cur, g)
        for fn in cur["late"]:
            fn()
        pend = []
        for ci in range(n_o):
            nxt = prep_head(ci + 1) if ci + 1 < n_o else None
            nl = nxt is not None and nxt["lat"]
            if nl:
                prep_R(nxt, 0)
            for g in range(4):
                if nl:
                    if g < 3:
                        prep_R(nxt, g + 1)
                    prep_G(nxt, g)
                main_G(cur, g)
                if pend:
                    pend.pop(0)()
                if nxt is not None and g in (1, 3) and nxt["late"]:
                    nxt["late"].pop(0)()
            while nxt is not None and nxt["late"]:
                nxt["late"].pop(0)()
            while pend:
                pend.pop(0)()
            if cur["lat"] and not fwd:
                k.dma(yb_s[b][cur["t0"]:cur["t0"] + 128, :], cur["ys"][:], reads=[cur["b_ys"]])
            if cur["lat"] and fwd:
                pend = post_parts(cur)
            cur = nxt
        while pend:
            pend.pop(0)()
        P.barrier()
        ph.close()

    def phase_Q(b):
        ph = contextlib.ExitStack()
        KTn, b_KTn = sb("Q_KTn", [128, NT], BF16, ph)
        KTr, b_KTr = sb("Q_KTr", [128, NT], BF16, ph)
        Vt, b_Vt = sb("Q_Vt", [128, NCH, 128], BF16, ph)
        wuv, b_wuv = sb("Q_wuv", [128, H, DV], BF16, ph)
        Qn_r = ring("Q_Qn", 2, [128, H, 512], BF16, ph)
        Qr_r = ring("Q_Qr", 2, [128, H, 512], BF16, ph)
        PT_r = ring("Q_PT", 6, [128, 512], BF16, ph)
        aD_r = ring("Q_aD", 2, [128, 512], F32, ph)
        PS_r = ring("Q_PS", 2, [128, 512], BF16, ph)
        rd_r = ring("Q_rd", 2, [128, 512], F32, ph)
        On_r = ring("Q_On", 2, [128, 512], BF16, ph)
        ym_r = ring("Q_ym", 2, [128, H, 512], BF16, ph)
        pS_r = ring("Q_pS", 4, [128, 512], F32, ph, psum=True)
        pO_r = ring("Q_pO", 2, [128, 512], F32, ph, psum=True)
        pD_r = ring("Q_pD", 1, [128, 512], F32, ph, psum=True)
        pU_r = ring("Q_pU", 1, [128, 512], F32, ph, psum=True)
        k.dma(KTn[:], KTn_s[b], writes=[b_KTn])
        k.dma(KTr[0:64, :], KTr_s[b], writes=[b_KTr])
        k.dma(KTr[64:128, :], KTr_s[b], writes=[b_KTr])
        k.dma(Vt[:], Vt_s[b], writes=[b_Vt])
        k.dma(wuv[:], wb["w_uv"][0], writes=[b_wuv])
        NP = NCH // 2

        def load_q(ti):
            t0 = ti * 512
            Qn, b_Qn = Qn_r.next()
            Qr, b_Qr = Qr_r.next()
            k.dma(Qn[:], Qn_s[b].rearrange("h p t -> p h t")[:, :, t0:t0 + 512], writes=[b_Qn])
            k.dma(Qr[0:64, :, :], Qr_s[b].rearrange("h p t -> p h t")[:, :, t0:t0 + 512], writes=[b_Qr])
            k.dma(Qr[64:128, :, :], Qr_s[b].rearrange("h p t -> p h t")[:, :, t0:t0 + 512], writes=[b_Qr])
            return Qn, b_Qn, Qr, b_Qr

        qnext = load_q(0)
        for ti in range(S // 512):
            t0 = ti * 512
            Qn, b_Qn, Qr, b_Qr = qnext
            if ti + 1 < S // 512:
                qnext = load_q(ti + 1)
            ym, b_ym = ym_r.next()
            steps = [(h, kp) for h in range(H) for kp in range(NP)]
            pSs = {}
            tails = []

            def emit_S(i):
                h, kp = steps[i]
                k0, k1 = 2 * kp, 2 * kp + 1
                pA, b_pA = pS_r.next()
                pB, b_pB = pS_r.next()
                k.mm([(pA[:], KTn[:, k0 * 128:(k0 + 1) * 128], Qn[:, h, :], True, False),
                      (pB[:], KTn[:, k1 * 128:(k1 + 1) * 128], Qn[:, h, :], True, False),
                      (pA[:], KTr[0:64, k0 * 128:(k0 + 1) * 128], Qr[0:64, h, :], False, True),
                      (pB[:], KTr[64:128, k1 * 128:(k1 + 1) * 128], Qr[64:128, h, :], False, True)],
                     [b_KTn, b_KTr, b_Qn, b_Qr], [b_pA, b_pB])
                pSs[i] = ((pA, b_pA), (pB, b_pB))

            emit_S(0)
            for i, (h, kp) in enumerate(steps):
                if i + 1 < len(steps):
                    emit_S(i + 1)
                (pA, b_pA), (pB, b_pB) = pSs.pop(i)
                PA, b_PA = PT_r.next()
                PB, b_PB = PT_r.next()
                k.act(PA[:], pA[:], AF.Exp, [b_pA], [b_PA], scale=ATTN_SCALE)
                k.act(PB[:], pB[:], AF.Exp, [b_pB], [b_PB], scale=ATTN_SCALE)
                PS, b_PS = PS_r.next()
                k.tt("dve", PS[:], PA[:], PB[:], ALU.add, [b_PA, b_PB], [b_PS])
                if kp == 0:
                    pO, b_pO = pO_r.next()
                    aD, b_aD = aD_r.next()
                    k.copy("dve", aD[:], PS[:], [b_PS], [b_aD])
                else:
                    k.tt("dve", aD[:], aD[:], PS[:], ALU.add, [b_aD, b_PS], [b_aD])
                k.mm([(pO[:], Vt[:, 2 * kp, :], PA[:], kp == 0, False),
                      (pO[:], Vt[:, 2 * kp + 1, :], PB[:], False, kp == NP - 1)], [b_Vt, b_PA, b_PB], [b_pO])
                if kp == NP - 1:
                    def tail1(h=h, pO=pO, b_pO=b_pO, aD=aD, b_aD=b_aD, due=i + 7):
                        pD, b_pD = pD_r.next()
                        k.mm([(pD[:], ones_f[:], aD[:], True, True)], [b_ones_f, b_aD], [b_pD])
                        rd, b_rd = rd_r.next()
                        k.recip(rd[:], pD[:], [b_pD], [b_rd])
                        On, b_On = On_r.next()
                        k.tt("dve", On[:], pO[:], rd[:], ALU.mult, [b_pO, b_rd], [b_On])

                        def tail2():
                            pU, b_pU = pU_r.next()
                            k.mm([(pU[:], wuv[:, h, :], On[:], True, True)], [b_wuv, b_On], [b_pU])
                            k.copy("act", ym[:, h, :], pU[:], [b_pU], [b_ym])
                        tails.append([due, 1, tail2])
                    tails.append([i + 2, 0, tail1])
                for tl_ in list(tails):
                    if tl_[0] <= i:
                        tails.remove(tl_)
                        tl_[2]()
            while tails:
                tl_ = tails.pop(0)
                tl_[2]()
            k.dma(ymT_s[b].rearrange("(h p) t -> p h t", p=128)[:, :, t0:t0 + 512], ym[:], reads=[b_ym])
        P.barrier()
        ph.close()

    def phase_T1(b):
        ph = contextlib.ExitStack()
        womla, b_womla = sb("T_womla", [128, KC, D], BF16, ph)
        wossd, b_wossd = sb("T_wossd", [128, 16, D], BF16, ph)
        wout, b_wout = sb("T_wout", [128, KC, D], BF16, ph)
        g1bc, b_g1bc = sb("T_g1bc", [128, D], F32, ph)
        ym_r = ring("T_ym", 2, [128, KC, 512], BF16, ph)
        yg_r = ring("T_yg", 1, [128, 16, 512], BF16, ph)
        gt_r = ring("T_gt", 1, [128, 16, 512], BF16, ph)
        xt_r = ring("T_xt", 2, [128, 4, D], F32, ph)
        mg_r = ring("T_mg", 2, [128, KC, 512], BF16, ph)
        m1_r = ring("T_m1", 2, [128, 512], F32, ph)
        m2_r = ring("T_m2", 2, [128, 512], F32, ph)
        tx_r = ring("T_tx", 2, [128, 512], F32, ph)
        acc_r = ring("T_acc", 6, [128, 512], F32, ph, psum=True)
        k.dma(womla[:], wb["w_o_mla"][0], writes=[b_womla])
        k.dma(wossd[:], wb["w_o_ssd"][0], writes=[b_wossd])
        k.dma(wout[:], wb["w_out"][0], writes=[b_wout])
        k.dma(g1bc[:], gb_s[0, b:b + 1, :].partition_broadcast(128), writes=[b_g1bc])
        def load_act(ti):
            t0 = ti * 512
            ym, b_ym = ym_r.next()
            yg, b_yg = yg_r.next()
            gt, b_gt = gt_r.next()
            k.dma(ym[:], ymT_s[b].rearrange("(c p) t -> p c t", p=128)[:, :, t0:t0 + 512], writes=[b_ym])
            k.dma(yg[:], ygT_s[b].rearrange("(c p) t -> p c t", p=128)[:, :, t0:t0 + 512], writes=[b_yg])
            k.dma(gt[:], gT_s[b].rearrange("(c p) t -> p c t", p=128)[:, :, t0:t0 + 512], writes=[b_gt])
            return ym, b_ym, yg, b_yg, gt, b_gt

        def load_x(ti):
            t0 = ti * 512
            xt, b_xt = xt_r.next()
            k.dma(xt[:], x_d[b][t0:t0 + 512, :].rearrange("(j p) d -> p j d", p=128), writes=[b_xt])
            return xt, b_xt

        nxt_act = load_act(0)
        nxt_x = load_x(0)
        for ti in range(S // 512):
            t0 = ti * 512
            ym, b_ym, yg, b_yg, gt, b_gt = nxt_act
            xt, b_xt = nxt_x
            mg, b_mg = mg_r.next()
            if ti + 1 < S // 512:
                nxt_x = load_x(ti + 1)
            for n_ in range(8):
                ns = slice(n_ * 128, (n_ + 1) * 128)
                pM, b_pM = acc_r.next()
                k.mm([(pM[:], womla[:, kc, ns], ym[:, kc, :], kc == 0, kc == KC - 1) for kc in range(KC)], [b_womla, b_ym], [b_pM])
                m1, b_m1 = m1_r.next()
                k.tt("dve", m1[:], pM[:], gt[:, n_, :], ALU.mult, [b_pM, b_gt], [b_m1])
                pS, b_pS = acc_r.next()
                k.mm([(pS[:], wossd[:, kc, ns], yg[:, kc, :], kc == 0, kc == 15) for kc in range(16)], [b_wossd, b_yg], [b_pS])
                m2, b_m2 = m2_r.next()
                k.tt("dve", m2[:], pS[:], gt[:, 8 + n_, :], ALU.mult, [b_pS, b_gt], [b_m2])
                k.tt("pool", mg[:, n_, :], m1[:], m2[:], ALU.add, [b_m1, b_m2], [b_mg])
            if ti + 1 < S // 512:
                nxt_act = load_act(ti + 1)
            for j in range(4):
                for nh in range(2):
                    cs_ = slice(nh * 512, (nh + 1) * 512)
                    pX, b_pX = acc_r.next()
                    k.mm([(pX[:], mg[:, kc, j * 128:(j + 1) * 128], wout[:, kc, cs_], kc == 0, kc == KC - 1) for kc in range(KC)],
                         [b_wout, b_mg], [b_pX])
                    tx, b_tx = tx_r.next()
                    k.tt("dve", tx[:], pX[:], g1bc[:, cs_], ALU.mult, [b_pX, b_g1bc], [b_tx])
                    k.tt("pool", xt[:, j, cs_], tx[:], xt[:, j, cs_], ALU.add, [b_tx, b_xt], [b_xt])
            k.dma(x1_s[b][t0:t0 + 512, :].rearrange("(j p) d -> p j d", p=128), xt[:], reads=[b_xt])
        P.barrier()
        ph.close()

    def phase_T2(b):
        ph = contextlib.ExitStack()
        wdn, b_wdn = sb("F_wdn", [128, NFF, D], BF16, ph)
        g2bc, b_g2bc = sb("F_g2bc", [128, D], F32, ph)
        wfin, b_wfin = sb("F_wfin", [128, D], F32, ph)
        xt_r = ring("F_xt", 2, [128, 4, D], F32, ph)
        xn_r = ring("F_xn", 1, [128, 4, D], BF16, ph)
        junk, b_junk = sb("F_junk", [128, D], BF16, ph)
        st_r = ring("F_st", 2, [128, 12], F32, ph)
        hT_r = ring("F_hT", 2, [128, KC, 512], BF16, ph)
        wf_r = ring("F_wf", 3, [128, KC, 512], BF16, ph)
        sg_r = ring("F_sg", 2, [128, 512], F32, ph)
        aT, b_aT = sb("F_aT", [128, NFF, 512], BF16, ph)
        tx_r = ring("F_tx", 2, [128, 512], F32, ph)
        x2_r = ring("F_x2", 1, [128, D], F32, ph)
        ot_r = ring("F_ot", 1, [128, D], F32, ph)
        fs_r = ring("F_fs", 2, [128, 4], F32, ph)
        acc_r = ring("F_acc", 6, [128, 512], F32, ph, psum=True)
        pT_r = ring("F_pT", 2, [128, 512], BF16, ph, psum=True)
        wfi = wb["w_ffn_in"][0]
        k.dma(wdn[:], wb["w_down"][0], writes=[b_wdn])
        k.dma(g2bc[:], gb_s[1, b:b + 1, :].partition_broadcast(128), writes=[b_g2bc])
        k.dma(wfin[:], wfinal_row_d.partition_broadcast(128), writes=[b_wfin])

        def stage1(ti):
            t0 = ti * 512
            xt, b_xt = xt_r.next()
            xn, b_xn = xn_r.next()
            st, b_st = st_r.next()
            hT, b_hT = hT_r.next()
            k.dma(xt[:], x1_s[b][t0:t0 + 512, :].rearrange("(j p) d -> p j d", p=128), writes=[b_xt])
            k.memset("dve", st[:], 0.0, [b_st])
            for j in range(4):
                k.act(junk[:], xt[:, j, :], AF.Square, [b_xt], [b_junk, b_st], accum=st[:, j:j + 1])
                rstd_op(st[:, 8 + j:9 + j], st[:, j:j + 1], D, 1, [b_st], [b_st])
                k.tt("pool", xn[:, j, :], xt[:, j, :], st[:, 8 + j:9 + j].to_broadcast([128, D]), ALU.mult, [b_xt, b_st], [b_xn])
            return dict(t0=t0, xt=xt, b_xt=b_xt, hT=hT, b_hT=b_hT, xn=xn, b_xn=b_xn)

        def stage1b(cx):
            hT = cx["hT"]; b_hT = cx["b_hT"]; xn = cx["xn"]; b_xn = cx["b_xn"]
            for kc in range(KC):
                pT, b_pT = pT_r.next()
                k.tr([(pT[:, j * 128:(j + 1) * 128], xn[:, j, kc * 128:(kc + 1) * 128]) for j in range(4)], ident_b[:],
                     [b_xn, b_ident_b], [b_pT])
                if kc % 2 == 0:
                    k.act(hT[:, kc, :], pT[:], AF.Identity, [b_pT, b_sc2e, b_modT], [b_hT],
                          scale=sc2e[:, kc, b:b + 1], bias=modT[:, 24 + kc, b:b + 1])
                else:
                    k.ts(hT[:, kc, :], pT[:], sc2e[:, kc, b:b + 1], modT[:, 24 + kc, b:b + 1], ALU.mult, ALU.add,
                         [b_pT, b_sc2e, b_modT], [b_hT])

        def stage2(cx, mid_hook):
            t0 = cx["t0"]; xt = cx["xt"]; b_xt = cx["b_xt"]; hT = cx["hT"]; b_hT = cx["b_hT"]
            for fg in range(11):
                wf, b_wf = get_wf()
                for i in range(2):
                    pG, b_pG = acc_r.next()
                    k.mm([(pG[:], wf[:, kc, i * 128:(i + 1) * 128], hT[:, kc, :], kc == 0, kc == KC - 1) for kc in range(KC)], [b_wf, b_hT], [b_pG])
                    pU, b_pU = acc_r.next()
                    k.mm([(pU[:], wf[:, kc, 256 + i * 128:256 + (i + 1) * 128], hT[:, kc, :], kc == 0, kc == KC - 1) for kc in range(KC)], [b_wf, b_hT], [b_pU])
                    sg, b_sg = sg_r.next()
                    k.act(sg[:], pG[:], AF.Silu, [b_pG], [b_sg])
                    k.tt("dve", aT[:, 2 * fg + i, :], pU[:], sg[:], ALU.mult, [b_pU, b_sg], [b_aT])
                if fg == 7:
                    mid_hook()
            fs, b_fs = fs_r.next()
            k.memset("dve", fs[:], 0.0, [b_fs])
            for j in range(4):
                x2, b_x2 = x2_r.next()
                for nh in range(2):
                    cs_ = slice(nh * 512, (nh + 1) * 512)
                    pD, b_pD = acc_r.next()
                    k.mm([(pD[:], aT[:, i, j * 128:(j + 1) * 128], wdn[:, i, cs_], i == 0, i == NFF - 1) for i in range(NFF)], [b_aT, b_wdn], [b_pD])
                    tx, b_tx = tx_r.next()
                    k.tt("dve", tx[:], pD[:], g2bc[:, cs_], ALU.mult, [b_pD, b_g2bc], [b_tx])
                    k.tt("pool", x2[:, cs_], tx[:], xt[:, j, cs_], ALU.add, [b_tx, b_xt], [b_x2])
                k.act(junk[:], x2[:], AF.Square, [b_x2], [b_junk, b_fs], accum=fs[:, 0:1])
                rstd_op(fs[:, 2:3], fs[:, 0:1], D, 1, [b_fs], [b_fs])
                ot, b_ot = ot_r.next()
                k.stt("dve", ot[:], x2[:], fs[:, 2:3], wfin[:], ALU.mult, ALU.mult, [b_x2, b_fs, b_wfin], [b_ot])
                P.out_dmas.append(k.dma(out_d[b][t0 + j * 128:t0 + (j + 1) * 128, :], ot[:], reads=[b_ot]))
                if j < 3:
                    fs, b_fs = fs_r.next()
                    k.memset("dve", fs[:], 0.0, [b_fs])

        n_t = S // 512
        issued = []
        gi = [0]

        def get_wf():
            idx = gi[0]
            gi[0] += 1
            while len(issued) < min(11 * n_t, idx + 3):
                fg = len(issued) % 11
                w_, bw_ = wf_r.next()
                k.dma(w_[:, :, 0:256], wfi[:, :, fg * 256:(fg + 1) * 256], writes=[bw_])
                k.dma(w_[:, :, 256:512], wfi[:, :, DFF + fg * 256:DFF + (fg + 1) * 256], writes=[bw_])
                issued.append((w_, bw_))
            return issued[idx]

        cxs = [None] * n_t
        cxs[0] = stage1(0)
        stage1b(cxs[0])
        for ti in range(n_t):
            if ti + 1 < n_t:
                cxs[ti + 1] = stage1(ti + 1)
                stage2(cxs[ti], lambda c_=cxs[ti + 1]: stage1b(c_))
            else:
                stage2(cxs[ti], lambda: None)
        P.barrier()
        ph.close()

    def dump_sbuf(name, tile, shape, dt):
        if name in dbg:
            t = nc.dram_tensor(name, list(shape), dt, kind="ExternalOutput").ap()
            k.dma(t, tile[:])

    stop_after = [x for x in dbg if x.startswith("stop")]
    stop_after = stop_after[0][4:] if stop_after else None

    def finish():
        P.barrier()
        P.emit()
        es.close()
        return nc

    for b in range(nseq):
        phase_A(b)
        if stop_after != "A":
            phase_C(b)
            if stop_after != "C":
                phase_S(b, 1)
                phase_S(b, 0)
        if stop_after in ("C", "S"):
            return finish()
        phase_Q(b)
        if stop_after == "Q":
            return finish()
        phase_T1(b)
        if stop_after == "T1":
            return finish()
        phase_T2(b)
        if stop_after == "A":
            dump_sbuf("dtv", dtv, [128, NCH, 64], F32)
            dump_sbuf("modT", modT, [128, 48, NB], F32)
            return finish()
    return finish()


_NSEQ = 2
_S = 4096


def kernel(**inputs):
    B = inputs["x"].shape[0]
    S = inputs["x"].shape[1]
    ncores = 8
    nseq = B // ncores
    nc = build_program(nseq, S)
    shared = prep_shared(inputs, S)
    in_maps = []
    for c in range(ncores):
        m = dict(shared)
        m.update(prep_core(inputs, c, nseq))
        in_maps.append(m)
    res = run_bass_kernel_spmd(nc, in_maps, core_ids=list(range(ncores)))
    out = np.concatenate([np.asarray(r["out"]) for r in res.results], axis=0)
    return out.astype(np.float32)
```
